# Optimizing a Trainium2 kernel written in Bass

```python
import jax
import jax.numpy as jnp
from jax import lax
import numpy as np

D_MODEL = 4096
BATCH = 2
SEQ = 8192
DEPTH = 2

GRID_W = 64
CTX_LEN = 256
ROPE_THETA = 10000.0
Q_BLOCK = 128
EPS = 1e-6

GQA_HEADS = 16
GQA_KV_HEADS = 4
GQA_HEAD_DIM = 128
MLA_HEADS = 16
MLA_Q_LORA = 1024
MLA_KV_LORA = 512
MLA_NOPE = 128
MLA_ROPE = 64
MLA_V = 128
ATT_MIX = GQA_HEADS * GQA_HEAD_DIM + MLA_HEADS * MLA_V
ATT_SIZES = [GQA_HEADS * GQA_HEAD_DIM, GQA_KV_HEADS * GQA_HEAD_DIM, GQA_KV_HEADS * GQA_HEAD_DIM,
             MLA_Q_LORA, MLA_KV_LORA, MLA_ROPE]
ATT_IN = sum(ATT_SIZES) + ATT_MIX

GLA_HEADS = 6
GLA_DK = 256
GLA_DV = 512
GLA_GATE_RANK = 16
GLA_GATE_TAU = 16.0
GLA_CHUNK = 64
FNET_GROUPS = 4
FNET_DIM = 256
REC_MIX = GLA_HEADS * GLA_DV + FNET_GROUPS * FNET_DIM
REC_SIZES = [GLA_HEADS * GLA_DK, GLA_HEADS * GLA_DK, GLA_HEADS * GLA_DV,
             GLA_GATE_RANK, GLA_GATE_RANK, FNET_GROUPS * FNET_DIM]
REC_IN = sum(REC_SIZES) + REC_MIX

N_EVEN = (DEPTH + 1) // 2
N_ODD = DEPTH // 2

kernel_name = 'hybrid_gqa_mla_gla_fnet_prefix_dit'

F32 = jnp.float32


def rms_norm(x, g):
    xf = x.astype(F32)
    y = xf * lax.rsqrt(jnp.mean(xf * xf, axis=-1, keepdims=True) + EPS)
    return (y * g.astype(F32)).astype(x.dtype)


def split_cols(p, sizes):
    return jnp.split(p, [int(s) for s in np.cumsum(sizes)], axis=-1)


def axial_rope(n, rot_dim):
    rows = n // GRID_W
    row = jnp.repeat(jnp.arange(rows, dtype=F32), GRID_W)
    col = jnp.tile(jnp.arange(GRID_W, dtype=F32), rows)
    quarter = rot_dim // 4
    inv_freq = ROPE_THETA ** (-jnp.arange(quarter, dtype=F32) / quarter)
    ang = jnp.concatenate([row[:, None] * inv_freq, col[:, None] * inv_freq], axis=-1)
    return jnp.cos(ang), jnp.sin(ang)


def apply_rope(x, rope):
    if rope is None:
        return x
    cos, sin = rope
    cos = cos[None, :, None, :]
    sin = sin[None, :, None, :]
    x1, x2 = jnp.split(x.astype(F32), 2, axis=-1)
    return jnp.concatenate([x1 * cos - x2 * sin, x1 * sin + x2 * cos], axis=-1).astype(x.dtype)


def block_attention(q, k, v, scale):
    b, sq, hkv, grp, dh = q.shape
    nb = sq // Q_BLOCK
    qb = q.reshape(b, nb, Q_BLOCK, hkv, grp, dh).transpose(1, 0, 2, 3, 4, 5)

    def one_block(qblk):
        s = jnp.einsum('bqhgd,bkhd->bhgqk', qblk, k, preferred_element_type=F32) * scale
        p = jax.nn.softmax(s, axis=-1).astype(v.dtype)
        return jnp.einsum('bhgqk,bkhd->bqhgd', p, v)

    o = lax.map(one_block, qb)
    return o.transpose(1, 0, 2, 3, 4, 5).reshape(b, sq, hkv * grp * v.shape[-1])


def attn_prep(h, w_in, qn_g, kn_g, cq_g, ckv_g, w_uq, w_ukv, rope_a, rope_b):
    b, s, _ = h.shape
    qa, ka, va, cq, ckv, kr, gate = split_cols(h @ w_in, ATT_SIZES)
    qa = apply_rope(rms_norm(qa.reshape(b, s, GQA_HEADS, GQA_HEAD_DIM), qn_g), rope_a)
    ka = apply_rope(rms_norm(ka.reshape(b, s, GQA_KV_HEADS, GQA_HEAD_DIM), kn_g), rope_a)
    va = va.reshape(b, s, GQA_KV_HEADS, GQA_HEAD_DIM)
    qb = (rms_norm(cq, cq_g) @ w_uq).reshape(b, s, MLA_HEADS, MLA_NOPE + MLA_ROPE)
    qb = jnp.concatenate([qb[..., :MLA_NOPE], apply_rope(qb[..., MLA_NOPE:], rope_b)], axis=-1)
    kv = (rms_norm(ckv, ckv_g) @ w_ukv).reshape(b, s, MLA_HEADS, MLA_NOPE + MLA_V)
    kr = apply_rope(kr.reshape(b, s, 1, MLA_ROPE), rope_b)
    kb = jnp.concatenate([kv[..., :MLA_NOPE], jnp.broadcast_to(kr, (b, s, MLA_HEADS, MLA_ROPE))], axis=-1)
    vb = kv[..., MLA_NOPE:]
    return qa, ka, va, qb, kb, vb, gate


def attn_mixer(h_lat, h_ctx, w_in, qn_g, kn_g, cq_g, ckv_g, w_uq, w_ukv, w_out, need_ctx):
    n = h_lat.shape[1]
    rope_a = axial_rope(n, GQA_HEAD_DIM)
    rope_b = axial_rope(n, MLA_ROPE)
    qa, ka, va, qb, kb, vb, g = attn_prep(h_lat, w_in, qn_g, kn_g, cq_g, ckv_g, w_uq, w_ukv, rope_a, rope_b)
    qa_c, ka_c, va_c, qb_c, kb_c, vb_c, g_c = attn_prep(h_ctx, w_in, qn_g, kn_g, cq_g, ckv_g, w_uq, w_ukv, None, None)
    group = GQA_HEADS // GQA_KV_HEADS

    def mix(qa, qb, ka, va, kb, vb, g):
        b, s = qa.shape[:2]
        oa = block_attention(qa.reshape(b, s, GQA_KV_HEADS, group, GQA_HEAD_DIM), ka, va, GQA_HEAD_DIM ** -0.5)
        ob = block_attention(qb.reshape(b, s, MLA_HEADS, 1, MLA_NOPE + MLA_ROPE), kb, vb,
                             (MLA_NOPE + MLA_ROPE) ** -0.5)
        y = jnp.concatenate([oa, ob], axis=-1) * jax.nn.silu(g)
        return y @ w_out

    out_lat = mix(qa, qb,
                  jnp.concatenate([ka, ka_c], axis=1), jnp.concatenate([va, va_c], axis=1),
                  jnp.concatenate([kb, kb_c], axis=1), jnp.concatenate([vb, vb_c], axis=1), g)
    out_ctx = mix(qa_c, qb_c, ka_c, va_c, kb_c, vb_c, g_c) if need_ctx else None
    return out_lat, out_ctx


def rec_prep(h, w_in, wg_f, bg_f, wg_b, bg_b):
    b, s, _ = h.shape
    q, k, v, gdf, gdb, u, gate = split_cols(h @ w_in, REC_SIZES)
    q = q.reshape(b, s, GLA_HEADS, GLA_DK) * (GLA_DK ** -0.5)
    k = k.reshape(b, s, GLA_HEADS, GLA_DK)
    v = v.reshape(b, s, GLA_HEADS, GLA_DV)

    def log_gate(gd, w, bias):
        z = (gd @ w + bias).astype(F32)
        return (jax.nn.log_sigmoid(z) / GLA_GATE_TAU).reshape(b, s, GLA_HEADS, GLA_DK)

    lf = log_gate(gdf, wg_f, bg_f)
    lb = log_gate(gdb, wg_b, bg_b)
    u = u.reshape(b, s, FNET_GROUPS, FNET_DIM)
    return q, k, v, lf, lb, u, gate


def gla_scan(q, k, v, lg, s0):
    b, s, h, dk = q.shape
    dv = v.shape[-1]
    nc = s // GLA_CHUNK

    def chunks(t):
        return t.reshape(b, nc, GLA_CHUNK, h, t.shape[-1]).transpose(1, 0, 3, 2, 4)

    mask = jnp.tril(jnp.ones((GLA_CHUNK, GLA_CHUNK), dtype=bool))

    def step(state, inp):
        qc, kc, vc, gc = inp
        qf = qc.astype(F32)
        kf = kc.astype(F32)
        vf = vc.astype(F32)
        cum = jnp.cumsum(gc, axis=2)
        o_inter = jnp.einsum('bhcd,bhde->bhce', qf * jnp.exp(cum), state)
        decay = jnp.exp(jnp.where(mask[:, :, None], cum[:, :, :, None, :] - cum[:, :, None, :, :], -jnp.inf))
        att = jnp.einsum('bhid,bhjd,bhijd->bhij', qf, kf, decay)
        o_intra = jnp.einsum('bhij,bhje->bhie', att, vf)
        last = cum[:, :, -1, :]
        state = jnp.exp(last)[..., None] * state + jnp.einsum(
            'bhcd,bhce->bhde', kf * jnp.exp(last[:, :, None, :] - cum), vf)
        return state, o_inter + o_intra

    state, o = lax.scan(step, s0, (chunks(q), chunks(k), chunks(v), chunks(lg)))
    o = o.transpose(1, 0, 3, 2, 4).reshape(b, s, h, dv).astype(v.dtype)
    return o, state


def gla_bidir(q, k, v, lf, lb, sf0, sb0):
    of, sf = gla_scan(q, k, v, lf, sf0)
    ob, sb = gla_scan(jnp.flip(q, 1), jnp.flip(k, 1), jnp.flip(v, 1), jnp.flip(lb, 1), sb0)
    return of + jnp.flip(ob, 1), sf, sb


def fourier_mix(u):
    return jnp.fft.fft2(u.astype(F32), axes=(1, 3), norm='ortho').real.astype(u.dtype)


def rec_mixer(h_lat, h_ctx, w_in, wg_f, bg_f, wg_b, bg_b, on_g, w_out, need_ctx):
    q, k, v, lf, lb, u, g = rec_prep(h_lat, w_in, wg_f, bg_f, wg_b, bg_b)
    qc, kc, vc, lfc, lbc, uc, gc = rec_prep(h_ctx, w_in, wg_f, bg_f, wg_b, bg_b)
    zero = jnp.zeros((h_ctx.shape[0], GLA_HEADS, GLA_DK, GLA_DV), F32)
    o_ctx, s_f, s_b = gla_bidir(qc, kc, vc, lfc, lbc, zero, zero)
    o_lat, _, _ = gla_bidir(q, k, v, lf, lb, s_f, s_b)

    def finish(o, u, g):
        b, s = o.shape[:2]
        o = rms_norm(o, on_g).reshape(b, s, GLA_HEADS * GLA_DV)
        f = fourier_mix(u).reshape(b, s, FNET_GROUPS * FNET_DIM)
        return (jnp.concatenate([o, f], axis=-1) * jax.nn.silu(g)) @ w_out

    out_lat = finish(o_lat, u, g)
    out_ctx = finish(o_ctx, uc, gc) if need_ctx else None
    return out_lat, out_ctx


def setup_inputs(seed: int = 0) -> dict:
    key = jax.random.key(seed)
    ks = iter(jax.random.split(key, 32))
    D = D_MODEL

    def nrm(shape, std):
        return jax.random.normal(next(ks), shape, jnp.float32) * std

    def gain(shape):
        return 1.0 + nrm(shape, 0.02)

    return {
        'x': nrm((BATCH, SEQ, D), 1.0),
        'c': nrm((BATCH, D), 1.0),
        'ctx': nrm((BATCH, CTX_LEN, D), 1.0),
        'c_ctx': nrm((D,), 1.0),
        'norm_g': gain((DEPTH, D)),
        'ada_w': nrm((DEPTH, D, 3 * D), 0.5 * D ** -0.5),
        'ada_b': nrm((DEPTH, 3 * D), 0.02),
        'att_w_in': nrm((N_EVEN, D, ATT_IN), D ** -0.5),
        'att_qn_g': gain((N_EVEN, GQA_HEAD_DIM)),
        'att_kn_g': gain((N_EVEN, GQA_HEAD_DIM)),
        'mla_cq_g': gain((N_EVEN, MLA_Q_LORA)),
        'mla_ckv_g': gain((N_EVEN, MLA_KV_LORA)),
        'mla_w_uq': nrm((N_EVEN, MLA_Q_LORA, MLA_HEADS * (MLA_NOPE + MLA_ROPE)), MLA_Q_LORA ** -0.5),
        'mla_w_ukv': nrm((N_EVEN, MLA_KV_LORA, MLA_HEADS * (MLA_NOPE + MLA_V)), MLA_KV_LORA ** -0.5),
        'att_w_out': nrm((N_EVEN, ATT_MIX, D), ATT_MIX ** -0.5),
        'rec_w_in': nrm((N_ODD, D, REC_IN), D ** -0.5),
        'gla_wg_f': nrm((N_ODD, GLA_GATE_RANK, GLA_HEADS * GLA_DK), GLA_GATE_RANK ** -0.5),
        'gla_bg_f': nrm((N_ODD, GLA_HEADS * GLA_DK), 0.1),
        'gla_wg_b': nrm((N_ODD, GLA_GATE_RANK, GLA_HEADS * GLA_DK), GLA_GATE_RANK ** -0.5),
        'gla_bg_b': nrm((N_ODD, GLA_HEADS * GLA_DK), 0.1),
        'gla_on_g': gain((N_ODD, GLA_DV)),
        'rec_w_out': nrm((N_ODD, REC_MIX, D), REC_MIX ** -0.5),
        'final_g': gain((D,)),
    }


def reference(x, c, ctx, c_ctx, norm_g, ada_w, ada_b, att_w_in, att_qn_g, att_kn_g, mla_cq_g, mla_ckv_g,
              mla_w_uq, mla_w_ukv, att_w_out, rec_w_in, gla_wg_f, gla_bg_f, gla_wg_b, gla_bg_b, gla_on_g,
              rec_w_out, final_g):
    x_ctx = ctx
    for layer in range(DEPTH):
        need_ctx = layer < DEPTH - 1
        mod = jax.nn.silu(c) @ ada_w[layer] + ada_b[layer]
        shift, scale, gate = jnp.split(mod[:, None, :], 3, axis=-1)
        mod_c = jax.nn.silu(c_ctx) @ ada_w[layer] + ada_b[layer]
        shift_c, scale_c, gate_c = jnp.split(mod_c, 3, axis=-1)
        h = rms_norm(x, norm_g[layer]) * (1.0 + scale) + shift
        hc = rms_norm(x_ctx, norm_g[layer]) * (1.0 + scale_c) + shift_c
        i = layer // 2
        if layer % 2 == 0:
            y, yc = attn_mixer(h, hc, att_w_in[i], att_qn_g[i], att_kn_g[i], mla_cq_g[i], mla_ckv_g[i],
                               mla_w_uq[i], mla_w_ukv[i], att_w_out[i], need_ctx)
        else:
            y, yc = rec_mixer(h, hc, rec_w_in[i], gla_wg_f[i], gla_bg_f[i], gla_wg_b[i], gla_bg_b[i],
                              gla_on_g[i], rec_w_out[i], need_ctx)
        x = x + gate * y
        if need_ctx:
            x_ctx = x_ctx + gate_c * yc
    return rms_norm(x, final_g)
```

```python
import contextlib
import numpy as np
import ml_dtypes
import concourse.bass as bass
import concourse.mybir as mybir
from concourse.bass_utils import run_bass_kernel_spmd

F32 = mybir.dt.float32; BF16 = mybir.dt.bfloat16
AF = mybir.ActivationFunctionType; ALU = mybir.AluOpType; AX = mybir.AxisListType
NPBF = ml_dtypes.bfloat16
EPS = 1e-6


class Dep:
    __slots__ = ("name", "lw", "rd", "dsem")
    def __init__(self, name=""):
        self.name = name; self.lw = None; self.rd = {}; self.dsem = None


class T:
    __slots__ = ("t", "d")
    def __init__(self, t, name):
        self.t = t; self.d = Dep(name)
    def __getitem__(self, k):
        return self.t[k]


class KB:
    ENG = ("pe", "act", "dve", "pool", "sp")
    def __init__(self, nc, stack):
        self.nc = nc; self.stack = stack
        self.ops = {e: [] for e in self.ENG}
        self.sem = {e: stack.enter_context(nc.semaphore("s_" + e)) for e in ("pe", "act", "dve", "pool")}
        self.allsems = {id(s): s for s in self.sem.values()}
        self.cnt = {}
        self.waited = {e: {} for e in self.ENG}
        self.nwaits = 0; self.ninst = 0
    def newsem(self, name):
        s = self.stack.enter_context(self.nc.semaphore(name))
        self.allsems[id(s)] = s
        return s
    def _need(self, eng, reads, writes, accumulate=False):
        need = {}
        def add(p):
            if p is None: return
            s, v = p
            k = id(s)
            if k not in need or need[k][1] < v: need[k] = (s, v)
        for r in reads:
            add(r.lw)
        for w in writes:
            if not accumulate: add(w.lw)
            for p in w.rd.values(): add(p)
        wd = self.waited[eng]
        for k, (s, v) in need.items():
            if wd.get(k, 0) < v:
                wd[k] = v
                self.ops[eng].append(lambda e, s=s, v=v: e.wait_ge(s, v))
                self.nwaits += 1
    def op(self, eng, fn, reads=(), writes=(), accumulate=False):
        reads = [r.d if isinstance(r, T) else r for r in reads]
        writes = [w.d if isinstance(w, T) else w for w in writes]
        self._need(eng, reads, writes, accumulate)
        s = self.sem[eng]; k = id(s)
        c = self.cnt.get(k, 0) + 1; self.cnt[k] = c
        self.ops[eng].append(lambda e, fn=fn, s=s: fn(e).then_inc(s, 1))
        self.ninst += 1
        for r in reads: r.rd[k] = (s, c)
        for w in writes:
            w.lw = (s, c); w.rd = {}
    def dma(self, q, pieces, reads=(), writes=(), track=None):
        reads = [r.d if isinstance(r, T) else r for r in reads]
        writes = [w.d if isinstance(w, T) else w for w in writes]
        self._need(q, reads, writes)
        t = track if track is not None else (writes[0] if writes else reads[0])
        if t.dsem is None: t.dsem = self.newsem("d_" + t.name)
        s = t.dsem; k = id(s)
        c = self.cnt.get(k, 0)
        for (o, i) in pieces:
            c += 16
            self.ops[q].append(lambda e, o=o, i=i, s=s: e.dma_start(out=o, in_=i).then_inc(s, 16))
            self.ninst += 1
        self.cnt[k] = c
        for r in reads: r.rd[k] = (s, c)
        for w in writes:
            w.lw = (s, c); w.rd = {}
    def wait_all(self, eng, deps):
        deps = [r.d if isinstance(r, T) else r for r in deps]
        self._need(eng, deps, deps)
    def barrier(self):
        for eng in self.ENG:
            wd = self.waited[eng]
            for k, s in self.allsems.items():
                v = self.cnt.get(k, 0)
                if v > 0 and wd.get(k, 0) < v:
                    wd[k] = v
                    self.ops[eng].append(lambda e, s=s, v=v: e.wait_ge(s, v))
    def emit(self, block):
        m = {"pe": block.tensor, "act": block.scalar, "dve": block.vector, "pool": block.gpsimd, "sp": block.sync}
        for en, dec in m.items():
            lst = self.ops[en]
            def f(e, lst=lst):
                for g in lst: g(e)
            dec(f)


class Prog:
    def __init__(self):
        self.nc = bass.Bass("TRN2", target_bir_lowering=False)
        self.st = contextlib.ExitStack()
        self.kb = KB(self.nc, self.st)
        self.outs = []
        self.n = 0
    def din(self, name, shape, dt=F32):
        return self.nc.dram_tensor(name, list(shape), dt, kind="ExternalInput").ap()
    def dout(self, name, shape, dt=F32):
        ap = self.nc.dram_tensor(name, list(shape), dt, kind="ExternalOutput").ap()
        d = Dep(name); self.outs.append(d)
        return ap, d
    def sb(self, name, shape, dt, stack=None):
        st = stack or self.st
        self.n += 1; name = f"{name}_{self.n}"
        return T(st.enter_context(self.nc.sbuf_tensor(name, list(shape), dt)), name)
    def ps(self, name, shape, dt, stack=None):
        st = stack or self.st
        self.n += 1; name = f"{name}_{self.n}"
        return T(st.enter_context(self.nc.psum_tensor(name, list(shape), dt)), name)
    def finish(self):
        self.kb.wait_all("sp", self.outs)
        with self.nc.Block() as block:
            self.kb.emit(block)
        self.st.close()
        return self.nc


def make_ident(P, kb, name="ident"):
    idf = P.sb(name + "f", [128, 128], F32)
    idb = P.sb(name, [128, 128], BF16)
    kb.op("pool", lambda e: e.memset(idf[:], 1.0), writes=[idf])
    kb.op("pool", lambda e: e.affine_select(out=idf[:], in_=idf[:], pattern=[[-1, 128]], compare_op=ALU.is_equal,
                                            fill=0.0, base=0, channel_multiplier=1), reads=[idf], writes=[idf])
    kb.op("dve", lambda e: e.tensor_copy(out=idb[:], in_=idf[:]), reads=[idf], writes=[idb])
    return idb, idf
D = 4096
def build_mod(ncols=3072):
    P = Prog(); kb = P.kb
    cT = P.din("cT", [D, 3])
    aw = P.din("aw", [D, ncols])
    ab = P.din("ab", [1, ncols])
    mo, mo_d = P.dout("mod", [3, ncols])
    ct = P.sb("ct", [128, 32, 3], F32); sc = P.sb("sc", [128, 32, 3], F32)
    wb = [P.sb(f"wb{i}", [128, 32, 512], F32) for i in range(2)]
    bt = P.sb("bt", [3, ncols], F32); ot = P.sb("ot", [3, ncols], F32)
    ps = [P.ps(f"ps{i}", [128, 512], F32) for i in range(2)]
    kb.dma("sp", [(ct[:], cT.rearrange("(c p) r -> p c r", p=128))], writes=[ct])
    kb.dma("sp", [(bt[:], ab.partition_broadcast(3))], writes=[bt])
    kb.op("act", lambda e: e.activation(out=sc[:], in_=ct[:], func=AF.Silu), reads=[ct], writes=[sc])
    nb = ncols // 512
    for j in range(nb):
        w = wb[j % 2]
        src = aw[:, j * 512:(j + 1) * 512].rearrange("(c p) n -> p c n", p=128)
        kb.dma("sp", [(w[:, 8 * i:8 * i + 8, :], src[:, 8 * i:8 * i + 8, :]) for i in range(4)], writes=[w])
        p = ps[j % 2]
        for kc in range(32):
            kb.op("pe", lambda e, p=p, w=w, kc=kc: e.matmul(p[0:3, :], sc[:, kc, :], w[:, kc, :], start=(kc == 0), stop=(kc == 31)),
                  reads=[sc, w], writes=[p], accumulate=(kc > 0))
        kb.op("dve", lambda e, p=p, j=j: e.tensor_tensor(out=ot[:, j * 512:(j + 1) * 512], in0=p[0:3, :], in1=bt[:, j * 512:(j + 1) * 512], op=ALU.add),
              reads=[p, bt], writes=[ot])
    kb.dma("sp", [(mo[:, :], ot[:])], reads=[ot], writes=[mo_d])
    return P.finish()

def run_mod(c, c_ctx, ada_w, ada_b):
    nc = build_mod()
    cT = np.ascontiguousarray(np.concatenate([c, c_ctx[None]], 0).T)
    awc = np.concatenate([ada_w[0], ada_w[1]], axis=1)
    abc = np.concatenate([ada_b[0], ada_b[1]], axis=0)[None]
    maps = [{"cT": cT, "aw": np.ascontiguousarray(awc[:, i * 3072:(i + 1) * 3072]), "ab": np.ascontiguousarray(abc[:, i * 3072:(i + 1) * 3072])} for i in range(8)]
    res = run_bass_kernel_spmd(nc, maps, core_ids=list(range(8)))
    mod = np.concatenate([r["mod"] for r in res.results], axis=1)
    return mod.reshape(3, 2, 3 * D)
NTOK = 2304
NT = NTOK // 128

def load_w(kb, wt, src, kch, pieces=4):
    ncols = src.shape[1]
    s3 = src.rearrange("(c p) n -> p c n", p=128)
    pieces = min(pieces, kch)
    step = kch // pieces
    wv = wt[:, 0:kch * ncols].rearrange("p (c n) -> p c n", c=kch)
    kb.dma("pool", [(wv[:, i * step:(i + 1) * step, :], s3[:, i * step:(i + 1) * step, :]) for i in range(pieces)], writes=[wt])
    r = T(wv, "wv"); r.d = wt.d
    return r

class Rot:
    def __init__(self, items): self.items = items; self.i = 0
    def next(self):
        x = self.items[self.i % len(self.items)]; self.i += 1
        return x

def rms_prep(P, kb, GT, g0, x_dram, modT, col_lat, hT, psr, identf, scope, G, Sh, evq):
    xt = [P.sb(f"xt{i}", [128, 4096], F32, scope) for i in range(2)]
    yt = P.sb("yt", [128, 4096], F32, scope)
    ss = P.sb("ss", [128, 2], F32, scope)
    for ti in range(GT):
        gt = g0 + ti
        x = xt[ti % 2]
        kb.dma("sp", [(x[:, 1024 * i:1024 * (i + 1)], x_dram[gt * 128:(gt + 1) * 128, 1024 * i:1024 * (i + 1)]) for i in range(4)], writes=[x])
        kb.op("act", lambda e, x=x: e.activation(out=yt[:], in_=x[:], func=AF.Square, accum_out=ss[:, 0:1]), reads=[x], writes=[yt, ss])
        kb.op("act", lambda e: e.activation(out=ss[:, 1:2], in_=ss[:, 0:1], func=AF.Sqrt, scale=1.0 / 4096, bias=EPS), reads=[ss], writes=[ss])
        kb.op("dve", lambda e: e.reciprocal(out=ss[:, 1:2], in_=ss[:, 1:2]), reads=[ss], writes=[ss])
        kb.op("dve", lambda e, x=x: e.tensor_scalar(out=yt[:], in0=x[:], scalar1=ss[:, 1:2], scalar2=None, op0=ALU.mult), reads=[x, ss], writes=[yt])
        sel = 0 if col_lat(gt) else 1
        for c4 in range(8):
            p = psr.next()
            for j in range(4):
                c = c4 * 4 + j
                kb.op("pe", lambda e, p=p, j=j, c=c: e.transpose(p[:, j * 128:(j + 1) * 128], yt[:, c * 128:(c + 1) * 128], identf[:]),
                      reads=[yt, identf], writes=[p], accumulate=(j > 0))
            for j in range(4):
                c = c4 * 4 + j
                eng = evq.next()
                o = hT[:, c, ti * 128:(ti + 1) * 128]
                if eng == "act":
                    kb.op("act", lambda e, o=o, p=p, j=j, c=c, sel=sel: e.activation(out=o, in_=p[:, j * 128:(j + 1) * 128], func=AF.Identity,
                          scale=G[:, sel, c:c + 1], bias=Sh[:, sel, c:c + 1]), reads=[p, G, Sh], writes=[hT])
                else:
                    kb.op("dve", lambda e, o=o, p=p, j=j, c=c, sel=sel: e.tensor_scalar(out=o, in0=p[:, j * 128:(j + 1) * 128],
                          scalar1=G[:, sel, c:c + 1], scalar2=Sh[:, sel, c:c + 1], op0=ALU.mult, op1=ALU.add), reads=[p, G, Sh], writes=[hT])

def head_norm_rope(P, kb, pm, nh, hd, gain_b, rope_t, ti, out_bf, tmp, do_norm=True, do_rope=True):
    half = hd // 2
    v3 = lambda t: t[:, 0:nh * hd].rearrange("p (h d) -> p h d", h=nh)
    if do_norm:
        sq, st = tmp["sq"], tmp["st"]
        kb.op("pool", lambda e: e.tensor_tensor(out=sq[:, 0:nh * hd], in0=pm[:, 0:nh * hd], in1=pm[:, 0:nh * hd], op=ALU.mult), reads=[pm], writes=[sq])
        kb.op("dve", lambda e: e.tensor_reduce(out=st[:, 0:nh], in_=v3(sq), axis=AX.X, op=ALU.add), reads=[sq], writes=[st])
        kb.op("act", lambda e: e.activation(out=st[:, 16:16 + nh], in_=st[:, 0:nh], func=AF.Sqrt, scale=1.0 / hd, bias=EPS), reads=[st], writes=[st])
        kb.op("dve", lambda e: e.reciprocal(out=st[:, 16:16 + nh], in_=st[:, 16:16 + nh]), reads=[st], writes=[st])
        kb.op("dve", lambda e: e.tensor_tensor(out=v3(pm), in0=v3(pm), in1=st[:, 16:16 + nh].unsqueeze(2).to_broadcast([128, nh, hd]), op=ALU.mult),
              reads=[pm, st], writes=[pm])
        kb.op("pool", lambda e: e.tensor_tensor(out=v3(pm), in0=v3(pm), in1=gain_b[:, 0:hd].unsqueeze(1).to_broadcast([128, nh, hd]), op=ALU.mult),
              reads=[pm, gain_b], writes=[pm])
    if not do_rope:
        kb.op("dve", lambda e: e.tensor_copy(out=out_bf, in_=pm[:, 0:nh * hd]), reads=[pm], writes=[tmp["outd"]])
        return
    t1, t2 = tmp["t1"], tmp["t2"]
    cosb = rope_t[:, ti, 0:half].unsqueeze(1).to_broadcast([128, nh, half])
    sinb = rope_t[:, ti, half:hd].unsqueeze(1).to_broadcast([128, nh, half])
    x1 = v3(pm)[:, :, 0:half]; x2 = v3(pm)[:, :, half:hd]
    o3 = out_bf.rearrange("p (h d) -> p h d", h=nh)
    t1a = t1[:, 0:nh * half].rearrange("p (h d) -> p h d", h=nh); t1b = t1[:, nh * half:2 * nh * half].rearrange("p (h d) -> p h d", h=nh)
    t2a = t2[:, 0:nh * half].rearrange("p (h d) -> p h d", h=nh); t2b = t2[:, nh * half:2 * nh * half].rearrange("p (h d) -> p h d", h=nh)
    kb.op("dve", lambda e: e.tensor_tensor(out=t1a, in0=x1, in1=cosb, op=ALU.mult), reads=[pm, rope_t], writes=[t1])
    kb.op("pool", lambda e: e.tensor_tensor(out=t2a, in0=x2, in1=sinb, op=ALU.mult), reads=[pm, rope_t], writes=[t2])
    kb.op("dve", lambda e: e.tensor_tensor(out=t1b, in0=x2, in1=cosb, op=ALU.mult), reads=[pm, rope_t], writes=[t1])
    kb.op("pool", lambda e: e.tensor_tensor(out=t2b, in0=x1, in1=sinb, op=ALU.mult), reads=[pm, rope_t], writes=[t2])
    kb.op("dve", lambda e: e.tensor_tensor(out=o3[:, :, 0:half], in0=t1a, in1=t2a, op=ALU.subtract), reads=[t1, t2], writes=[tmp["outd"]])
    kb.op("pool", lambda e: e.tensor_tensor(out=o3[:, :, half:hd], in0=t1b, in1=t2b, op=ALU.add), reads=[t1, t2], writes=[tmp["outd"]])

def build_a0(GT=6):
    P = Prog(); kb = P.kb
    NG = NT // GT
    GTOK = GT * 128
    nchunks = [(i, min(512, GTOK - i)) for i in range(0, GTOK, 512)]
    xin = P.din("xin", [NTOK, 4096])
    modT = P.din("modT", [128, 5, 32])
    w_in = P.din("w_in", [4096, 8768])
    gains = P.din("gains", [1, 128 + 128 + 1024 + 512])
    w_uq = P.din("w_uq", [1024, 3072]); w_ukv = P.din("w_ukv", [512, 4096])
    ropeA = P.din("ropeA", [NTOK, 128]); ropeB = P.din("ropeB", [NTOK, 64])
    qaT, qaT_d = P.dout("qaT", [16, 128, NTOK], BF16); kaT, kaT_d = P.dout("kaT", [4, 128, NTOK], BF16)
    va, va_d = P.dout("va", [NTOK, 512], BF16)
    qbnT, qbnT_d = P.dout("qbnT", [16, 128, NTOK], BF16); qbrT, qbrT_d = P.dout("qbrT", [8, 128, NTOK], BF16)
    kbnT, kbnT_d = P.dout("kbnT", [16, 128, NTOK], BF16); krT, krT_d = P.dout("krT", [64, NTOK], BF16)
    vb, vb_d = P.dout("vb", [NTOK, 2048], BF16); gT, gT_d = P.dout("gT", [32, 128, NTOK], BF16)

    identb, identf = make_ident(P, kb)
    mt = P.sb("mt", [128, 5, 32], F32); G = P.sb("G", [128, 2, 32], F32); Sh = P.sb("Sh", [128, 2, 32], F32)
    kb.dma("sp", [(mt[:], modT)], writes=[mt])
    for sel in range(2):
        kb.op("dve", lambda e, sel=sel: e.tensor_scalar(out=G[:, sel, :], in0=mt[:, 1 + 2 * sel, :], scalar1=1.0, scalar2=None, op0=ALU.add), reads=[mt], writes=[G])
        kb.op("dve", lambda e, sel=sel: e.tensor_tensor(out=G[:, sel, :], in0=G[:, sel, :], in1=mt[:, 0, :], op=ALU.mult), reads=[mt, G], writes=[G])
        kb.op("dve", lambda e, sel=sel: e.tensor_copy(out=Sh[:, sel, :], in_=mt[:, 2 + 2 * sel, :]), reads=[mt], writes=[Sh])
    gb = P.sb("gb", [128, 1792], F32)
    kb.dma("sp", [(gb[:], gains.partition_broadcast(128))], writes=[gb])
    gq = T(gb.t, "gq"); gq.d = gb.d
    hT = P.sb("hT", [128, 32, GTOK], BF16)
    cqT = P.sb("cqT", [128, 8, GTOK], BF16); ckvT = P.sb("ckvT", [128, 4, GTOK], BF16)
    psm = Rot([P.ps(f"psm{i}", [128, 512], F32) for i in range(6)])
    pst = Rot([P.ps(f"pst{i}", [128, 512], BF16) for i in range(2)])
    evq = Rot(["act", "dve"])
    col_lat = lambda gt: gt < 16
    for g in range(NG):
        g0 = g * GT; tok0 = g0 * 128
        with contextlib.ExitStack() as sc1:
            rms_prep(P, kb, GT, g0, xin, modT, col_lat, hT, psm, identf, sc1, G, Sh, evq)
            kb.barrier()
        with contextlib.ExitStack() as sc2:
            wbuf = Rot([P.sb(f"wbuf{i}", [128, 16384], BF16, sc2) for i in range(2)])
            pm = Rot([P.sb(f"pm{i}", [128, 1024], F32, sc2) for i in range(2)])
            tmp = {"sq": P.sb("sq", [128, 1024], F32, sc2), "st": P.sb("st", [128, 32], F32, sc2),
                   "t1": P.sb("t1", [128, 512], F32, sc2), "t2": P.sb("t2", [128, 512], F32, sc2)}
            rA = P.sb("rA", [128, GT, 128], F32, sc2); rB = P.sb("rB", [128, GT, 64], F32, sc2)
            kb.dma("sp", [(rA[:], ropeA[tok0:tok0 + GTOK, :].rearrange("(t p) d -> p t d", p=128))], writes=[rA])
            kb.dma("sp", [(rB[:], ropeB[tok0:tok0 + GTOK, :].rearrange("(t p) d -> p t d", p=128))], writes=[rB])
            tmb = Rot([P.sb(f"tmb{i}", [128, 1024], BF16, sc2) for i in range(2)])
            fms = Rot([P.sb(f"fms{i}", [128, GTOK], BF16, sc2) for i in range(3)])
            trs = Rot([P.sb(f"trs{i}", [128, 512], BF16, sc2) for i in range(3)])

            def tm_block(wt, ncols, ti, p):
                for kc in range(32):
                    kb.op("pe", lambda e, kc=kc: e.matmul(p[:, 0:ncols], hT[:, kc, ti * 128:(ti + 1) * 128], wt[:, kc, 0:ncols], start=(kc == 0), stop=(kc == 31)),
                          reads=[hT, wt], writes=[p], accumulate=(kc > 0))

            def transpose_out(src_bf, nblk, dst_fn, dst_dep, src_dep, rows=128):
                p = pst.next()
                for j in range(nblk):
                    kb.op("pe", lambda e, j=j: e.transpose(p[:, j * 128:(j + 1) * 128], src_bf[:, j * 128:(j + 1) * 128], identb[:]),
                          reads=[src_dep, identb], writes=[p], accumulate=(j > 0))
                s = trs.next()
                eng = evq.next()
                if eng == "act":
                    kb.op("act", lambda e: e.activation(out=s[:, 0:nblk * 128], in_=p[:, 0:nblk * 128], func=AF.Copy), reads=[p], writes=[s])
                else:
                    kb.op("dve", lambda e: e.tensor_copy(out=s[:, 0:nblk * 128], in_=p[:, 0:nblk * 128]), reads=[p], writes=[s])
                dst_fn(s)

            blocks = [("qa", 0, 512, 0), ("qa", 512, 512, 1), ("qa", 1024, 512, 2), ("qa", 1536, 512, 3), ("ka", 2048, 512, 0),
                      ("va", 2560, 512, 0), ("ckv", 4096, 512, 0), ("kr", 4608, 64, 0)]
            nxt = load_w(kb, wbuf.next(), w_in[:, blocks[0][1]:blocks[0][1] + blocks[0][2]], 32)
            for bi, (kind, c0, ncols, idx) in enumerate(blocks):
                wt = nxt
                if bi + 1 < len(blocks):
                    nxt = load_w(kb, wbuf.next(), w_in[:, blocks[bi + 1][1]:blocks[bi + 1][1] + blocks[bi + 1][2]], 32)
                for ti in range(GT):
                    gt = g0 + ti
                    p = psm.next()
                    tm_block(wt, ncols, ti, p)
                    if kind in ("qa", "ka"):
                        m = pm.next()
                        kb.op("act", lambda e, m=m, p=p: e.activation(out=m[:, 0:512], in_=p[:, :], func=AF.Copy), reads=[p], writes=[m])
                        ob = tmb.next(); tmp["outd"] = ob.d
                        gain = gb[:, 0:128] if kind == "qa" else gb[:, 128:256]
                        gt_ = T(gain, "g"); gt_.d = gb.d
                        head_norm_rope(P, kb, m, 4, 128, gt_, rA, ti, ob[:, 0:512], tmp)
                        if kind == "qa":
                            dst = lambda s, idx=idx, gt=gt: kb.dma("sp", [(qaT[idx * 4:idx * 4 + 4, :, gt * 128:(gt + 1) * 128].rearrange("h d t -> d h t"),
                                                                         s[:, 0:512].rearrange("d (h t) -> d h t", h=4))], reads=[s], writes=[qaT_d])
                        else:
                            dst = lambda s, gt=gt: kb.dma("sp", [(kaT[0:4, :, gt * 128:(gt + 1) * 128].rearrange("h d t -> d h t"),
                                                                  s[:, 0:512].rearrange("d (h t) -> d h t", h=4))], reads=[s], writes=[kaT_d])
                        transpose_out(ob, 4, dst, None, ob)
                    elif kind == "va":
                        ob = tmb.next()
                        eng = evq.next()
                        if eng == "act":
                            kb.op("act", lambda e, ob=ob, p=p: e.activation(out=ob[:, 0:512], in_=p[:, :], func=AF.Copy), reads=[p], writes=[ob])
                        else:
                            kb.op("dve", lambda e, ob=ob, p=p: e.tensor_copy(out=ob[:, 0:512], in_=p[:, :]), reads=[p], writes=[ob])
                        kb.dma("sp", [(va[gt * 128:(gt + 1) * 128, :], ob[:, 0:512])], reads=[ob], writes=[va_d])
                    elif kind == "ckv":
                        m = pm.next()
                        kb.op("act", lambda e, m=m, p=p: e.activation(out=m[:, 0:512], in_=p[:, :], func=AF.Copy), reads=[p], writes=[m])
                        ob = tmb.next(); tmp["outd"] = ob.d
                        gt_ = T(gb[:, 1280:1792], "g"); gt_.d = gb.d
                        head_norm_rope(P, kb, m, 1, 512, gt_, None, ti, ob[:, 0:512], tmp, do_rope=False)
                        def dst(s, ti=ti):
                            kb.op("pool", lambda e: e.tensor_copy(out=ckvT[:, :, ti * 128:(ti + 1) * 128], in_=s[:, 0:512].rearrange("d (c t) -> d c t", c=4)), reads=[s], writes=[ckvT])
                        transpose_out(ob, 4, dst, None, ob)
                    elif kind == "kr":
                        m = pm.next()
                        kb.op("act", lambda e, m=m, p=p: e.activation(out=m[:, 0:64], in_=p[:, 0:64], func=AF.Copy), reads=[p], writes=[m])
                        ob = tmb.next(); tmp["outd"] = ob.d
                        kb.op("pool", lambda e, ob=ob: e.memset(ob[:, 0:128], 0.0), writes=[ob])
                        head_norm_rope(P, kb, m, 1, 64, None, rB, ti, ob[:, 0:64], tmp, do_norm=False)
                        dst = lambda s, gt=gt: kb.dma("sp", [(krT[:, gt * 128:(gt + 1) * 128], s[0:64, 0:128])], reads=[s], writes=[krT_d])
                        transpose_out(ob, 1, dst, None, ob)
            wa = load_w(kb, wbuf.next(), w_in[:, 3072:3584], 32)
            wb_ = load_w(kb, wbuf.next(), w_in[:, 3584:4096], 32)
            for ti in range(GT):
                m = pm.next()
                for half, wt in enumerate((wa, wb_)):
                    p = psm.next()
                    tm_block(wt, 512, ti, p)
                    kb.op("act", lambda e, m=m, p=p, half=half: e.activation(out=m[:, half * 512:(half + 1) * 512], in_=p[:, :], func=AF.Copy), reads=[p], writes=[m])
                ob = tmb.next(); tmp["outd"] = ob.d
                gt_ = T(gb[:, 256:1280], "g"); gt_.d = gb.d
                head_norm_rope(P, kb, m, 1, 1024, gt_, None, ti, ob[:, 0:1024], tmp, do_rope=False)
                for hh in range(2):
                    def dst(s, ti=ti, hh=hh):
                        kb.op("pool", lambda e: e.tensor_copy(out=cqT[:, hh * 4:hh * 4 + 4, ti * 128:(ti + 1) * 128], in_=s[:, 0:512].rearrange("d (c t) -> d c t", c=4)), reads=[s], writes=[cqT])
                    obh = T(ob[:, hh * 512:(hh + 1) * 512], "obh"); obh.d = ob.d
                    transpose_out(obh, 4, dst, None, ob)
            nxt = load_w(kb, wbuf.next(), w_in[:, 4672:4672 + 512], 32)
            for wbi in range(8):
                wt = nxt
                if wbi + 1 < 8:
                    nxt = load_w(kb, wbuf.next(), w_in[:, 4672 + (wbi + 1) * 512:4672 + (wbi + 2) * 512], 32)
                for sbk in range(4):
                    blk = wbi * 4 + sbk
                    s = fms.next()
                    for (n0, n) in nchunks:
                        p = psm.next()
                        for kc in range(32):
                            kb.op("pe", lambda e, kc=kc, p=p, n0=n0, n=n, sbk=sbk, wt=wt: e.matmul(p[:, 0:n], wt[:, kc, sbk * 128:(sbk + 1) * 128], hT[:, kc, n0:n0 + n], start=(kc == 0), stop=(kc == 31)),
                                  reads=[hT, wt], writes=[p], accumulate=(kc > 0))
                        kb.op("act", lambda e, p=p, s=s, n0=n0, n=n: e.activation(out=s[:, n0:n0 + n], in_=p[:, 0:n], func=AF.Silu), reads=[p], writes=[s])
                    kb.dma("sp", [(gT[blk, :, tok0:tok0 + GTOK], s[:, :])], reads=[s], writes=[gT_d])
            nxt = load_w(kb, wbuf.next(), w_uq[:, 0:768], 8)
            for hb in range(4):
                wt = nxt
                if hb + 1 < 4:
                    nxt = load_w(kb, wbuf.next(), w_uq[:, (hb + 1) * 768:(hb + 2) * 768], 8)
                else:
                    nxt = load_w(kb, wbuf.next(), w_ukv[:, 0:512], 4)
                for hh in range(4):
                    h = hb * 4 + hh
                    s = fms.next()
                    for (n0, n) in nchunks:
                        p = psm.next()
                        for kc in range(8):
                            kb.op("pe", lambda e, kc=kc, p=p, n0=n0, n=n, hh=hh, wt=wt: e.matmul(p[:, 0:n], wt[:, kc, hh * 192:hh * 192 + 128], cqT[:, kc, n0:n0 + n], start=(kc == 0), stop=(kc == 7)),
                                  reads=[cqT, wt], writes=[p], accumulate=(kc > 0))
                        eng = evq.next()
                        if eng == "act":
                            kb.op("act", lambda e, p=p, s=s, n0=n0, n=n: e.activation(out=s[:, n0:n0 + n], in_=p[:, 0:n], func=AF.Copy), reads=[p], writes=[s])
                        else:
                            kb.op("dve", lambda e, p=p, s=s, n0=n0, n=n: e.tensor_copy(out=s[:, n0:n0 + n], in_=p[:, 0:n]), reads=[p], writes=[s])
                    kb.dma("sp", [(qbnT[h, :, tok0:tok0 + GTOK], s[:, :])], reads=[s], writes=[qbnT_d])
                for ti in range(GT):
                    gt = g0 + ti
                    p = psm.next()
                    for kc in range(8):
                        kb.op("pe", lambda e, kc=kc, p=p, ti=ti, wt=wt: e.matmul(p[:, 0:256], cqT[:, kc, ti * 128:(ti + 1) * 128],
                              wt[:, kc, :].rearrange("p (h d) -> p h d", h=4)[:, :, 128:192], start=(kc == 0), stop=(kc == 7)),
                              reads=[cqT, wt], writes=[p], accumulate=(kc > 0))
                    m = pm.next()
                    kb.op("act", lambda e, m=m, p=p: e.activation(out=m[:, 0:256], in_=p[:, 0:256], func=AF.Copy), reads=[p], writes=[m])
                    ob = tmb.next(); tmp["outd"] = ob.d
                    head_norm_rope(P, kb, m, 4, 64, None, rB, ti, ob[:, 0:256], tmp, do_norm=False)
                    dst = lambda s, hb=hb, gt=gt: kb.dma("sp", [(qbrT[hb * 2:hb * 2 + 2, :, gt * 128:(gt + 1) * 128].rearrange("h d t -> d h t"),
                                                               s[:, 0:256].rearrange("d (h t) -> d h t", h=2))], reads=[s], writes=[qbrT_d])
                    transpose_out(ob, 2, dst, None, ob)
            for hb in range(8):
                wt = nxt
                if hb + 1 < 8:
                    nxt = load_w(kb, wbuf.next(), w_ukv[:, (hb + 1) * 512:(hb + 2) * 512], 4)
                for hh in range(2):
                    h = hb * 2 + hh
                    s = fms.next()
                    for (n0, n) in nchunks:
                        p = psm.next()
                        for kc in range(4):
                            kb.op("pe", lambda e, kc=kc, p=p, n0=n0, n=n, hh=hh, wt=wt: e.matmul(p[:, 0:n], wt[:, kc, hh * 256:hh * 256 + 128], ckvT[:, kc, n0:n0 + n], start=(kc == 0), stop=(kc == 3)),
                                  reads=[ckvT, wt], writes=[p], accumulate=(kc > 0))
                        eng = evq.next()
                        if eng == "act":
                            kb.op("act", lambda e, p=p, s=s, n0=n0, n=n: e.activation(out=s[:, n0:n0 + n], in_=p[:, 0:n], func=AF.Copy), reads=[p], writes=[s])
                        else:
                            kb.op("dve", lambda e, p=p, s=s, n0=n0, n=n: e.tensor_copy(out=s[:, n0:n0 + n], in_=p[:, 0:n]), reads=[p], writes=[s])
                    kb.dma("sp", [(kbnT[h, :, tok0:tok0 + GTOK], s[:, :])], reads=[s], writes=[kbnT_d])
                for ti in range(GT):
                    gt = g0 + ti
                    p = psm.next()
                    for kc in range(4):
                        kb.op("pe", lambda e, kc=kc, p=p, ti=ti, wt=wt: e.matmul(p[:, 0:256], ckvT[:, kc, ti * 128:(ti + 1) * 128],
                              wt[:, kc, :].rearrange("p (h d) -> p h d", h=2)[:, :, 128:256], start=(kc == 0), stop=(kc == 3)),
                              reads=[ckvT, wt], writes=[p], accumulate=(kc > 0))
                    ob = tmb.next()
                    eng = evq.next()
                    if eng == "act":
                        kb.op("act", lambda e, ob=ob, p=p: e.activation(out=ob[:, 0:256], in_=p[:, 0:256], func=AF.Copy), reads=[p], writes=[ob])
                    else:
                        kb.op("dve", lambda e, ob=ob, p=p: e.tensor_copy(out=ob[:, 0:256], in_=p[:, 0:256]), reads=[p], writes=[ob])
                    kb.dma("sp", [(vb[gt * 128:(gt + 1) * 128, hb * 256:(hb + 1) * 256], ob[:, 0:256])], reads=[ob], writes=[vb_d])
            kb.barrier()
    print("A0 ninst", kb.ninst, "nwaits", kb.nwaits)
    return P.finish()
GRID_W = 64
def rope_tables(tok_idx, rot_dim):
    q = rot_dim // 4
    inv = (np.float32(10000.0) ** (-np.arange(q, dtype=np.float32) / np.float32(q))).astype(np.float32)
    row = (tok_idx // GRID_W).astype(np.float32); col = (tok_idx % GRID_W).astype(np.float32)
    ang = np.concatenate([row[:, None] * inv[None], col[:, None] * inv[None]], axis=1).astype(np.float32)
    return np.concatenate([np.cos(ang), np.sin(ang)], axis=1).astype(np.float32)

def core_rope(t, rot_dim):
    lat = rope_tables(np.arange(t * 2048, (t + 1) * 2048), rot_dim)
    ctx = np.concatenate([np.ones((256, rot_dim // 2), np.float32), np.zeros((256, rot_dim // 2), np.float32)], axis=1)
    return np.ascontiguousarray(np.concatenate([lat, ctx], axis=0))

def pc(v):
    return np.ascontiguousarray(v.reshape(32, 128).T)

def a0_maps(x, ctx, mod, inp):
    maps = []
    for core in range(8):
        b, t = core // 4, core % 4
        xin = np.ascontiguousarray(np.concatenate([x[b, t * 2048:(t + 1) * 2048], ctx[b]], axis=0))
        m_lat = mod[b, 0]; m_ctx = mod[2, 0]
        modT = np.stack([pc(inp["norm_g"][0]), pc(m_lat[4096:8192]), pc(m_lat[0:4096]), pc(m_ctx[4096:8192]), pc(m_ctx[0:4096])], axis=1)
        gains = np.concatenate([inp["att_qn_g"][0], inp["att_kn_g"][0], inp["mla_cq_g"][0], inp["mla_ckv_g"][0]])[None]
        maps.append({"xin": xin, "modT": np.ascontiguousarray(modT), "w_in": inp["att_w_in"][0], "gains": np.ascontiguousarray(gains),
                     "w_uq": inp["mla_w_uq"][0], "w_ukv": inp["mla_w_ukv"][0], "ropeA": core_rope(t, 128), "ropeB": core_rope(t, 64)})
    return maps
NKEY = 8448; NKB = NKEY // 128

def outproj_residual(P, kb, yT_src, w_out, x_in, gmod, x_out, x_out_d, psm, sel_fn, GT=9, ntiles=NT, ngm=2):
    NG = ntiles // GT; GTOK = GT * 128
    with contextlib.ExitStack() as sc:
        yTs = P.sb("yTs", [128, 32, GTOK], BF16, sc)
        wbuf = Rot([P.sb(f"wo{i}", [128, 16384], BF16, sc) for i in range(2)])
        gm = P.sb("gm", [128, 2, 4096], F32, sc)
        xt = Rot([P.sb(f"xr{i}", [128, 512], F32, sc) for i in range(3)])
        tm = Rot([P.sb(f"tm{i}", [128, 512], F32, sc) for i in range(2)])
        xo = Rot([P.sb(f"xo{i}", [128, 512], F32, sc) for i in range(3)])
        kb.dma("sp", [(gm[:, s, :], gmod[s:s + 1, :].partition_broadcast(128)) for s in range(ngm)], writes=[gm])
        for g in range(NG):
            tok0 = g * GTOK
            kb.dma("sp", [(yTs[:, 8 * i:8 * i + 8, :], yT_src(8 * i, 8 * i + 8, tok0, GTOK)) for i in range(4)], writes=[yTs])
            nxt = load_w(kb, wbuf.next(), w_out[:, 0:512], 32)
            for cb in range(8):
                wt = nxt
                if cb + 1 < 8:
                    nxt = load_w(kb, wbuf.next(), w_out[:, (cb + 1) * 512:(cb + 2) * 512], 32)
                for ti in range(GT):
                    gt = g * GT + ti
                    x = xt.next()
                    kb.dma("sp", [(x[:], x_in[gt * 128:(gt + 1) * 128, cb * 512:(cb + 1) * 512])], writes=[x])
                    p = psm.next()
                    for kc in range(32):
                        kb.op("pe", lambda e, kc=kc, p=p, ti=ti, wt=wt: e.matmul(p[:, :], yTs[:, kc, ti * 128:(ti + 1) * 128], wt[:, kc, :], start=(kc == 0), stop=(kc == 31)),
                              reads=[yTs, wt], writes=[p], accumulate=(kc > 0))
                    t = tm.next(); o = xo.next()
                    sel = sel_fn(gt)
                    kb.op("dve", lambda e, t=t, p=p, sel=sel, cb=cb: e.tensor_tensor(out=t[:], in0=p[:, :], in1=gm[:, sel, cb * 512:(cb + 1) * 512], op=ALU.mult), reads=[p, gm], writes=[t])
                    kb.op("pool", lambda e, t=t, o=o, x=x: e.tensor_tensor(out=o[:], in0=t[:], in1=x[:], op=ALU.add), reads=[t, x], writes=[o])
                    kb.dma("sp", [(x_out[gt * 128:(gt + 1) * 128, cb * 512:(cb + 1) * 512], o[:])], reads=[o], writes=[x_out_d])
        kb.barrier()

def build_b0():
    P = Prog(); kb = P.kb
    qaT = P.din("qaT", [16, 128, NTOK], BF16); qbnT = P.din("qbnT", [16, 128, NTOK], BF16); qbrT = P.din("qbrT", [8, 128, NTOK], BF16)
    KaT = P.din("KaT", [4, 128, NKEY], BF16); Va = P.din("Va", [NKEY, 512], BF16)
    KbnT = P.din("KbnT", [16, 128, NKEY], BF16); KrT = P.din("KrT", [64, NKEY], BF16); Vb = P.din("Vb", [NKEY, 2048], BF16)
    gT = P.din("gT", [32, 128, NTOK], BF16)
    w_out = P.din("w_out", [4096, 4096]); x_in = P.din("x_in", [NTOK, 4096]); gmod = P.din("gmod", [2, 4096])
    x1, x1_d = P.dout("x1", [NTOK, 4096])
    yT = P.nc.dram_tensor("yT_scr", [32, 128, NTOK], BF16, kind="Internal").ap(); yT_d = Dep("yT")
    psm = Rot([P.ps(f"psm{i}", [128, 512], F32) for i in range(3)])
    pso = Rot([P.ps(f"pso{i}", [128, 512], F32) for i in range(2)])
    psr = P.ps("psr", [128, 512], F32)
    qtiles = [(0, 512, 0, NKB), (512, 512, 0, NKB), (1024, 512, 0, NKB), (1536, 512, 0, NKB), (2048, 256, 64, NKB)]
    with contextlib.ExitStack() as sc:
        ones = P.sb("ones", [128, 128], F32, sc)
        kb.op("pool", lambda e: e.memset(ones[:], 1.0), writes=[ones])
        Kt = Rot([P.sb(f"Kt{i}", [128, NKEY], BF16, sc) for i in range(2)])
        Vt = Rot([P.sb(f"Vt{i}", [128, NKB, 128], BF16, sc) for i in range(2)])
        Krt = P.sb("Krt", [128, NKEY], BF16, sc)
        Qt = Rot([P.sb(f"Qt{i}", [128, NTOK], BF16, sc) for i in range(2)])
        Qrt = Rot([P.sb(f"Qrt{i}", [128, NTOK], BF16, sc) for i in range(2)])
        Gt = Rot([P.sb(f"Gt{i}", [128, NTOK], BF16, sc) for i in range(2)])
        pT = Rot([P.sb(f"pT{i}", [128, 512], BF16, sc) for i in range(4)])
        accA = Rot([P.sb(f"accA{i}", [128, 512], F32, sc) for i in range(2)])
        accB = Rot([P.sb(f"accB{i}", [128, 512], F32, sc) for i in range(2)])
        accs = P.sb("accs", [128, 512], F32, sc); rinv = P.sb("rinv", [128, 512], F32, sc)
        yf = Rot([P.sb(f"yf{i}", [128, 512], F32, sc) for i in range(2)])
        yb = Rot([P.sb(f"yb{i}", [128, 512], BF16, sc) for i in range(2)])
        kb.dma("sp", [(Krt[0:64, :], KrT[:, :]), (Krt[64:128, :], KrT[:, :])], writes=[Krt])

        def attend(head, K, V, Q, Qr, half, scale):
            G = Gt.next()
            kb.dma("sp", [(G[:], gT[head, :, :])], writes=[G])
            def do_tile(q0, qn, kb0, kb1):
                po = pso.next()
                aA = accA.next(); aB = accB.next()
                nblk = kb1 - kb0
                for i, kbi in enumerate(range(kb0, kb1)):
                    ps_ = psm.next()
                    kb.op("pe", lambda e, ps_=ps_, kbi=kbi: e.matmul(ps_[:, 0:qn], K[:, kbi * 128:(kbi + 1) * 128], Q[:, q0:q0 + qn], start=True, stop=(Qr is None)),
                          reads=[K, Q], writes=[ps_])
                    if Qr is not None:
                        r0 = 64 * half
                        kb.op("pe", lambda e, ps_=ps_, kbi=kbi, r0=r0: e.matmul(ps_[:, 0:qn], Krt[r0:r0 + 64, kbi * 128:(kbi + 1) * 128], Qr[r0:r0 + 64, q0:q0 + qn], start=False, stop=True),
                              reads=[Krt, Qr], writes=[ps_], accumulate=True)
                    pt = pT.next()
                    kb.op("act", lambda e, ps_=ps_, pt=pt: e.activation(out=pt[:, 0:qn], in_=ps_[:, 0:qn], func=AF.Exp, scale=scale), reads=[ps_], writes=[pt])
                    acc, eng = (aA, "pool") if i % 2 == 0 else (aB, "dve")
                    if i < 2:
                        kb.op(eng, lambda e, acc=acc, pt=pt: e.tensor_copy(out=acc[:, 0:qn], in_=pt[:, 0:qn]), reads=[pt], writes=[acc])
                    else:
                        kb.op(eng, lambda e, acc=acc, pt=pt: e.tensor_tensor(out=acc[:, 0:qn], in0=acc[:, 0:qn], in1=pt[:, 0:qn], op=ALU.add), reads=[pt, acc], writes=[acc])
                    kb.op("pe", lambda e, po=po, kbi=kbi, pt=pt, i=i: e.matmul(po[:, 0:qn], V[:, kbi, :], pt[:, 0:qn], start=(i == 0), stop=(i == nblk - 1)),
                          reads=[V, pt], writes=[po], accumulate=(i > 0))
                kb.op("dve", lambda e, aA=aA, aB=aB: e.tensor_tensor(out=accs[:, 0:qn], in0=aA[:, 0:qn], in1=aB[:, 0:qn], op=ALU.add), reads=[aA, aB], writes=[accs])
                kb.op("pe", lambda e: e.matmul(psr[:, 0:qn], ones[:, :], accs[:, 0:qn], start=True, stop=True), reads=[ones, accs], writes=[psr])
                kb.op("dve", lambda e: e.reciprocal(out=rinv[:, 0:qn], in_=psr[:, 0:qn]), reads=[psr], writes=[rinv])
                y1 = yf.next(); y2 = yb.next()
                kb.op("dve", lambda e, po=po, y1=y1: e.tensor_tensor(out=y1[:, 0:qn], in0=po[:, 0:qn], in1=rinv[:, 0:qn], op=ALU.mult), reads=[po, rinv], writes=[y1])
                kb.op("pool", lambda e, y1=y1, y2=y2, G=G: e.tensor_tensor(out=y2[:, 0:qn], in0=y1[:, 0:qn], in1=G[:, q0:q0 + qn], op=ALU.mult), reads=[y1, G], writes=[y2])
                kb.dma("sp", [(yT[head, :, q0:q0 + qn], y2[:, 0:qn])], reads=[y2], writes=[yT_d])
            for qt in qtiles:
                do_tile(*qt)

        for kvh in range(4):
            K = Kt.next(); V = Vt.next()
            kb.dma("sp", [(K[:, i * 2112:(i + 1) * 2112], KaT[kvh, :, i * 2112:(i + 1) * 2112]) for i in range(4)], writes=[K])
            kb.dma("sp", [(V[:, :, :], Va[:, kvh * 128:(kvh + 1) * 128].rearrange("(k p) d -> p k d", p=128))], writes=[V])
            for qh in range(4):
                head = kvh * 4 + qh
                Q = Qt.next()
                kb.dma("sp", [(Q[:], qaT[head, :, :])], writes=[Q])
                attend(head, K, V, Q, None, 0, 128.0 ** -0.5)
        for h in range(16):
            K = Kt.next(); V = Vt.next()
            kb.dma("sp", [(K[:, i * 2112:(i + 1) * 2112], KbnT[h, :, i * 2112:(i + 1) * 2112]) for i in range(4)], writes=[K])
            kb.dma("sp", [(V[:, :, :], Vb[:, h * 128:(h + 1) * 128].rearrange("(k p) d -> p k d", p=128))], writes=[V])
            Q = Qt.next()
            kb.dma("sp", [(Q[:], qbnT[h, :, :])], writes=[Q])
            if h % 2 == 0:
                Qr = Qrt.next()
                kb.dma("sp", [(Qr[:], qbrT[h // 2, :, :])], writes=[Qr])
            attend(16 + h, K, V, Q, Qr, h % 2, 192.0 ** -0.5)
        kb.barrier()
    def yT_src(b0, b1, tok0, n):
        return yT[b0:b1, :, tok0:tok0 + n].rearrange("b p t -> p b t")
    psm2 = Rot([psm.items[0], psm.items[1], psm.items[2], pso.items[0], pso.items[1], psr])
    kb.wait_all("sp", [yT_d])
    outproj_residual(P, kb, yT_src, w_out, x_in, gmod, x1, x1_d, psm2, lambda gt: 0 if gt < 16 else 1)
    print("B0 ninst", kb.ninst, "nwaits", kb.nwaits)
    return P.finish()

def b0_maps(a0o, x, ctx, mod, inp):
    maps = []
    for core in range(8):
        b, t = core // 4, core % 4
        o = a0o[core]
        def gk(name, axis):
            parts = [np.take(a0o[b * 4 + tt][name], np.arange(0, 2048), axis=axis) for tt in range(4)]
            parts.append(np.take(a0o[b * 4][name], np.arange(2048, 2304), axis=axis))
            return np.ascontiguousarray(np.concatenate(parts, axis=axis))
        xin = np.ascontiguousarray(np.concatenate([x[b, t * 2048:(t + 1) * 2048], ctx[b]], axis=0))
        gm = np.ascontiguousarray(np.stack([mod[b, 0, 8192:], mod[2, 0, 8192:]]))
        maps.append({"qaT": o["qaT"], "qbnT": o["qbnT"], "qbrT": o["qbrT"], "gT": o["gT"],
                     "KaT": gk("kaT", 2), "Va": gk("va", 0), "KbnT": gk("kbnT", 2), "KrT": gk("krT", 1), "Vb": gk("vb", 0),
                     "w_out": inp["att_w_out"][0], "x_in": xin, "gmod": gm})
    return maps
def build_a1(GT=6):
    P = Prog(); kb = P.kb
    NG = NT // GT; GTOK = GT * 128
    nchunks = [(i, min(512, GTOK - i)) for i in range(0, GTOK, 512)]
    xin = P.din("xin", [NTOK, 4096])
    modT = P.din("modT", [128, 5, 32])
    w_in = P.din("w_in", [4096, 11296])
    dftc = P.din("dftc", [256, 512], BF16)
    qT, qT_d = P.dout("qT", [12, 128, NTOK], BF16); kT, kT_d = P.dout("kT", [12, 128, NTOK], BF16)
    v, v_d = P.dout("v", [NTOK, 3072], BF16); gdT, gdT_d = P.dout("gdT", [32, NTOK], F32)
    ucs, ucs_d = P.dout("ucs", [NTOK, 4, 512], BF16)
    sgl, sgl_d = P.dout("sgl", [NTOK, 3072], BF16); sgfT, sgfT_d = P.dout("sgfT", [8, 128, NTOK], BF16)
    identb, identf = make_ident(P, kb)
    mt = P.sb("mt", [128, 5, 32], F32); G = P.sb("G", [128, 2, 32], F32); Sh = P.sb("Sh", [128, 2, 32], F32)
    kb.dma("sp", [(mt[:], modT)], writes=[mt])
    for sel in range(2):
        kb.op("dve", lambda e, sel=sel: e.tensor_scalar(out=G[:, sel, :], in0=mt[:, 1 + 2 * sel, :], scalar1=1.0, scalar2=None, op0=ALU.add), reads=[mt], writes=[G])
        kb.op("dve", lambda e, sel=sel: e.tensor_tensor(out=G[:, sel, :], in0=G[:, sel, :], in1=mt[:, 0, :], op=ALU.mult), reads=[mt, G], writes=[G])
        kb.op("dve", lambda e, sel=sel: e.tensor_copy(out=Sh[:, sel, :], in_=mt[:, 2 + 2 * sel, :]), reads=[mt], writes=[Sh])
    dc = P.sb("dc", [128, 2, 512], BF16)
    kb.dma("sp", [(dc[:], dftc.rearrange("(c p) n -> p c n", p=128))], writes=[dc])
    hT = P.sb("hT", [128, 32, GTOK], BF16)
    uT = P.sb("uT", [128, 8, GTOK], BF16)
    psm = Rot([P.ps(f"psm{i}", [128, 512], F32) for i in range(6)])
    evq = Rot(["act", "dve"])
    for g in range(NG):
        g0 = g * GT; tok0 = g0 * 128
        with contextlib.ExitStack() as sc1:
            rms_prep(P, kb, GT, g0, xin, modT, lambda gt: gt < 16, hT, psm, identf, sc1, G, Sh, evq)
            kb.barrier()
        with contextlib.ExitStack() as sc2:
            wbuf = Rot([P.sb(f"wbuf{i}", [128, 16384], BF16, sc2) for i in range(2)])
            tmb = Rot([P.sb(f"tmb{i}", [128, 512], BF16, sc2) for i in range(3)])
            fms = Rot([P.sb(f"fms{i}", [128, GTOK], BF16, sc2) for i in range(3)])
            gds = P.sb("gds", [32, GTOK], F32, sc2)

            def evac(o, i, reads, writes, func=AF.Copy):
                eng = evq.next() if func == AF.Copy else "act"
                if eng == "act":
                    kb.op("act", lambda e: e.activation(out=o, in_=i, func=func), reads=reads, writes=writes)
                else:
                    kb.op("dve", lambda e: e.tensor_copy(out=o, in_=i), reads=reads, writes=writes)

            def fm_cols(wt, c0, M, dst_fn, func=AF.Copy, stage=None):
                s = stage if stage is not None else fms.next()
                for (n0, n) in nchunks:
                    p = psm.next()
                    for kc in range(32):
                        kb.op("pe", lambda e, kc=kc, p=p, n0=n0, n=n: e.matmul(p[0:M, 0:n], wt[:, kc, c0:c0 + M], hT[:, kc, n0:n0 + n], start=(kc == 0), stop=(kc == 31)),
                              reads=[hT, wt], writes=[p], accumulate=(kc > 0))
                    evac(s[0:M, n0:n0 + n], p[0:M, 0:n], [p], [s], func)
                dst_fn(s)

            def tm_cols(wt, ncols, dst, dst_d, c_out, func=AF.Copy):
                for ti in range(GT):
                    gt = g0 + ti
                    p = psm.next()
                    for kc in range(32):
                        kb.op("pe", lambda e, kc=kc, p=p, ti=ti: e.matmul(p[:, 0:ncols], hT[:, kc, ti * 128:(ti + 1) * 128], wt[:, kc, 0:ncols], start=(kc == 0), stop=(kc == 31)),
                              reads=[hT, wt], writes=[p], accumulate=(kc > 0))
                    ob = tmb.next()
                    evac(ob[:, 0:ncols], p[:, 0:ncols], [p], [ob], func)
                    kb.dma("sp", [(dst[gt * 128:(gt + 1) * 128, c_out:c_out + ncols], ob[:, 0:ncols])], reads=[ob], writes=[dst_d])

            blocks = [("q", 0, 512), ("q", 512, 512), ("q", 1024, 512), ("k", 1536, 512), ("k", 2048, 512), ("k", 2560, 512)] + \
                     [("v", 3072 + 512 * i, 512) for i in range(6)] + [("gd", 6144, 32), ("u", 6176, 512), ("u", 6688, 512)] + \
                     [("sgl", 7200 + 512 * i, 512) for i in range(6)] + [("sgf", 10272, 512), ("sgf", 10784, 512)]
            nxt = load_w(kb, wbuf.next(), w_in[:, blocks[0][1]:blocks[0][1] + blocks[0][2]], 32)
            cnt = {"q": 0, "k": 0, "v": 0, "u": 0, "sgl": 0, "sgf": 0}
            for bi, (kind, c0, ncols) in enumerate(blocks):
                wt = nxt
                if bi + 1 < len(blocks):
                    nxt = load_w(kb, wbuf.next(), w_in[:, blocks[bi + 1][1]:blocks[bi + 1][1] + blocks[bi + 1][2]], 32)
                if kind in ("q", "k", "sgf"):
                    dstT, dstT_d = {"q": (qT, qT_d), "k": (kT, kT_d), "sgf": (sgfT, sgfT_d)}[kind]
                    for sbk in range(4):
                        blk = cnt[kind]; cnt[kind] += 1
                        fm_cols(wt, sbk * 128, 128, lambda s, blk=blk, dstT=dstT, dstT_d=dstT_d: kb.dma("sp", [(dstT[blk, :, tok0:tok0 + GTOK], s[:, :])], reads=[s], writes=[dstT_d]),
                                func=(AF.Silu if kind == "sgf" else AF.Copy))
                elif kind == "u":
                    for sbk in range(4):
                        blk = cnt["u"]; cnt["u"] += 1
                        uview = T(uT[:, blk, :], "uv"); uview.d = uT.d
                        fm_cols(wt, sbk * 128, 128, lambda s: None, stage=uview)
                elif kind == "gd":
                    fm_cols(wt, 0, 32, lambda s: kb.dma("sp", [(gdT[:, tok0:tok0 + GTOK], s[0:32, :])], reads=[s], writes=[gdT_d]), stage=gds)
                elif kind == "v":
                    tm_cols(wt, 512, v, v_d, cnt["v"] * 512); cnt["v"] += 1
                elif kind == "sgl":
                    tm_cols(wt, 512, sgl, sgl_d, cnt["sgl"] * 512, func=AF.Silu); cnt["sgl"] += 1
            for ti in range(GT):
                gt = g0 + ti
                for grp in range(4):
                    p = psm.next()
                    for cc in range(2):
                        kb.op("pe", lambda e, cc=cc, p=p, ti=ti, grp=grp: e.matmul(p[:, :], uT[:, 2 * grp + cc, ti * 128:(ti + 1) * 128], dc[:, cc, :], start=(cc == 0), stop=(cc == 1)),
                              reads=[uT, dc], writes=[p], accumulate=(cc > 0))
                    ob = tmb.next()
                    evac(ob[:, :], p[:, :], [p], [ob])
                    kb.dma("sp", [(ucs[gt * 128:(gt + 1) * 128, grp, :], ob[:, :])], reads=[ob], writes=[ucs_d])
            kb.barrier()
    print("A1 ninst", kb.ninst, "nwaits", kb.nwaits)
    return P.finish()

def dft_c_table():
    c = np.arange(256)
    ang = 2 * np.pi * ((c[:, None] * c[None, :]) % 256) / 256.0
    return np.concatenate([np.cos(ang), np.sin(ang)], axis=1).astype(NPBF)

def a1_maps(x1o, mod, inp):
    maps = []
    tab = dft_c_table()
    for core in range(8):
        b, t = core // 4, core % 4
        m_lat = mod[b, 1]; m_ctx = mod[2, 1]
        modT = np.stack([pc(inp["norm_g"][1]), pc(m_lat[4096:8192]), pc(m_lat[0:4096]), pc(m_ctx[4096:8192]), pc(m_ctx[0:4096])], axis=1)
        maps.append({"xin": x1o[core], "modT": np.ascontiguousarray(modT), "w_in": inp["rec_w_in"][0], "dftc": tab})
    return maps
NSEQ = 8448; SEGT = 6; NSEG = 66 // SEGT
def build_b1():
    P = Prog(); kb = P.kb
    qT = P.din("qT", [3, 2, 128, NSEQ], BF16); kT = P.din("kT", [3, 2, 128, NSEQ], BF16)
    v = P.din("v", [3, NSEQ, 512], BF16); gd = P.din("gd", [3, 16, NSEQ], F32)
    wg = P.din("wg", [3, 16, 256], F32); bg = P.din("bg", [3, 128, 2], F32)
    o, o_d = P.dout("o", [3, 8192, 512], F32)
    identb, identf = make_ident(P, kb)
    mask = P.sb("mask", [128, 128], F32); zeros = P.sb("zeros", [128, 128], F32)
    kb.op("pool", lambda e: e.memset(mask[:], 1.0), writes=[mask])
    kb.op("pool", lambda e: e.affine_select(out=mask[:], in_=mask[:], pattern=[[1, 128]], compare_op=ALU.is_ge, fill=0.0, base=0, channel_multiplier=-1), reads=[mask], writes=[mask])
    kb.op("pool", lambda e: e.memset(zeros[:], 0.0), writes=[zeros])
    pz = P.ps("pz", [128, 512], F32); patt = P.ps("patt", [128, 512], F32); pkd = P.ps("pkd", [128, 512], BF16)
    po = Rot([P.ps(f"po{i}", [128, 512], F32) for i in range(2)])
    pdS = [P.ps(f"pdS{i}", [128, 512], F32) for i in range(2)]
    SCALE = 256.0 ** -0.5
    class Pair: pass
    prs = []
    for j in range(3):
        pr = Pair(); pr.j = j
        pr.S = P.sb(f"S{j}", [128, 2, 512], F32); pr.Sb = P.sb(f"Sb{j}", [128, 2, 512], BF16)
        kb.op("pool", lambda e, pr=pr: e.memset(pr.S[:], 0.0), writes=[pr.S])
        kb.op("pool", lambda e, pr=pr: e.memset(pr.Sb[:], 0.0), writes=[pr.Sb])
        pr.wg = P.sb(f"wg{j}", [16, 256], F32); pr.bg = P.sb(f"bg{j}", [128, 2], F32)
        kb.dma("sp", [(pr.wg[:], wg[j, :, :])], writes=[pr.wg]); kb.dma("sp", [(pr.bg[:], bg[j, :, :])], writes=[pr.bg])
        kb.op("dve", lambda e, pr=pr: e.tensor_scalar(out=pr.bg[:], in0=pr.bg[:], scalar1=-1.0, scalar2=None, op0=ALU.mult), reads=[pr.bg], writes=[pr.bg])
        pr.qs = Rot([P.sb(f"qs{j}_{i}", [128, 2, SEGT * 128], BF16) for i in range(2)])
        pr.ks = Rot([P.sb(f"ks{j}_{i}", [128, 2, SEGT * 128], BF16) for i in range(2)])
        pr.vs = Rot([P.sb(f"vs{j}_{i}", [128, SEGT, 512], BF16) for i in range(2)])
        pr.gs = Rot([P.sb(f"gs{j}_{i}", [16, SEGT * 128], F32) for i in range(2)])
        for nm, shp, dt in (("e1", [128, 256], F32), ("l1", [128, 256], F32), ("cumn", [128, 256], F32), ("eq", [128, 256], F32), ("ek", [128, 256], F32),
                            ("ed", [128, 256], F32), ("sm", [128, 4], F32), ("qe", [128, 256], BF16), ("ke", [128, 256], BF16), ("kdT", [128, 256], BF16),
                            ("kd", [128, 256], BF16), ("attT", [128, 128], BF16)):
            setattr(pr, nm, P.sb(f"{nm}{j}", shp, dt))
        pr.osb = Rot([P.sb(f"osb{j}_{i}", [128, 512], F32) for i in range(2)])
        prs.append(pr)

    def tile_step(pr, gt, tk, qs, ks, vs, gs):
        j = pr.j
        sl = slice(tk * 128, (tk + 1) * 128)
        for c in range(2):
            kb.op("pe", lambda e, c=c: e.matmul(pz[:, c * 128:(c + 1) * 128], pr.wg[:, c * 128:(c + 1) * 128], gs[:, sl], start=True, stop=True), reads=[pr.wg, gs], writes=[pz], accumulate=(c > 0))
        for c in range(2):
            kb.op("act", lambda e, c=c: e.activation(out=pr.e1[:, c * 128:(c + 1) * 128], in_=pz[:, c * 128:(c + 1) * 128], func=AF.Exp, scale=-1.0, bias=pr.bg[:, c:c + 1]), reads=[pz, pr.bg], writes=[pr.e1])
        kb.op("act", lambda e: e.activation(out=pr.l1[:], in_=pr.e1[:], func=AF.Ln, scale=1.0, bias=1.0), reads=[pr.e1], writes=[pr.l1])
        for c in range(2):
            kb.op("dve", lambda e, c=c: e.tensor_tensor_scan(out=pr.cumn[:, c * 128:(c + 1) * 128], data0=zeros[:], data1=pr.l1[:, c * 128:(c + 1) * 128], initial=0.0, op0=ALU.add, op1=ALU.add),
                  reads=[zeros, pr.l1], writes=[pr.cumn])
        lastn = pr.cumn[:].rearrange("p (c t) -> p c t", c=2)[:, :, 127]
        kb.op("dve", lambda e: e.tensor_scalar(out=pr.sm[:, 0:2], in0=lastn, scalar1=-1.0 / 16, scalar2=None, op0=ALU.mult), reads=[pr.cumn], writes=[pr.sm])
        kb.op("act", lambda e: e.activation(out=pr.sm[:, 2:4], in_=pr.sm[:, 0:2], func=AF.Exp), reads=[pr.sm], writes=[pr.sm])
        kb.op("act", lambda e: e.activation(out=pr.eq[:], in_=pr.cumn[:], func=AF.Exp, scale=-1.0 / 16), reads=[pr.cumn], writes=[pr.eq])
        kb.op("act", lambda e: e.activation(out=pr.ek[:], in_=pr.cumn[:], func=AF.Exp, scale=1.0 / 16), reads=[pr.cumn], writes=[pr.ek])
        for c in range(2):
            kb.op("act", lambda e, c=c: e.activation(out=pr.ed[:, c * 128:(c + 1) * 128], in_=pr.cumn[:, c * 128:(c + 1) * 128], func=AF.Exp, scale=1.0 / 16, bias=pr.sm[:, c:c + 1]), reads=[pr.cumn, pr.sm], writes=[pr.ed])
        v3 = lambda t: t[:].rearrange("p (c t) -> p c t", c=2)
        kb.op("dve", lambda e: e.scalar_tensor_tensor(out=v3(pr.qe), in0=qs[:, :, sl], scalar=SCALE, in1=v3(pr.eq), op0=ALU.mult, op1=ALU.mult), reads=[qs, pr.eq], writes=[pr.qe])
        kb.op("pool", lambda e: e.tensor_tensor(out=v3(pr.ke), in0=ks[:, :, sl], in1=v3(pr.ek), op=ALU.mult), reads=[ks, pr.ek], writes=[pr.ke])
        kb.op("pool", lambda e: e.tensor_tensor(out=v3(pr.kdT), in0=ks[:, :, sl], in1=v3(pr.ed), op=ALU.mult), reads=[ks, pr.ed], writes=[pr.kdT])
        for c in range(2):
            kb.op("pe", lambda e, c=c: e.transpose(pkd[:, c * 128:(c + 1) * 128], pr.kdT[:, c * 128:(c + 1) * 128], identb[:]), reads=[pr.kdT, identb], writes=[pkd], accumulate=(c > 0))
        kb.op("act", lambda e: e.activation(out=pr.kd[:], in_=pkd[:, 0:256], func=AF.Copy), reads=[pkd], writes=[pr.kd])
        for c in range(2):
            kb.op("pe", lambda e, c=c: e.matmul(patt[:, 0:128], pr.ke[:, c * 128:(c + 1) * 128], pr.qe[:, c * 128:(c + 1) * 128], start=(c == 0), stop=(c == 1)), reads=[pr.ke, pr.qe], writes=[patt], accumulate=(c > 0))
        kb.op("dve", lambda e: e.tensor_tensor(out=pr.attT[:], in0=patt[:, 0:128], in1=mask[:], op=ALU.mult), reads=[patt, mask], writes=[pr.attT])
        p = po.next()
        for c in range(2):
            kb.op("pe", lambda e, c=c: e.matmul(p[:, :], pr.qe[:, c * 128:(c + 1) * 128], pr.Sb[:, c, :], start=(c == 0), stop=False), reads=[pr.qe, pr.Sb], writes=[p], accumulate=(c > 0))
        kb.op("pe", lambda e: e.matmul(p[:, :], pr.attT[:], vs[:, tk, :], start=False, stop=True), reads=[pr.attT, vs], writes=[p], accumulate=True)
        if gt >= 2:
            ob = pr.osb.next()
            kb.op("act", lambda e: e.activation(out=ob[:], in_=p[:, :], func=AF.Copy), reads=[p], writes=[ob])
            kb.dma("sp", [(o[j, (gt - 2) * 128:(gt - 1) * 128, :], ob[:])], reads=[ob], writes=[o_d])
        for c in range(2):
            kb.op("pe", lambda e, c=c: e.matmul(pdS[c][:, :], pr.kd[:, c * 128:(c + 1) * 128], vs[:, tk, :], start=True, stop=True), reads=[pr.kd, vs], writes=[pdS[c]])
            kb.op("dve", lambda e, c=c: e.scalar_tensor_tensor(out=pr.S[:, c, :], in0=pr.S[:, c, :], scalar=pr.sm[:, 2 + c:3 + c], in1=pdS[c][:, :], op0=ALU.mult, op1=ALU.add), reads=[pr.S, pr.sm, pdS[c]], writes=[pr.S])
        kb.op("pool", lambda e: e.tensor_copy(out=pr.Sb[:], in_=pr.S[:]), reads=[pr.S], writes=[pr.Sb])

    for seg in range(NSEG):
        t0 = seg * SEGT * 128; n = SEGT * 128
        cur = []
        for pr in prs:
            j = pr.j
            qs = pr.qs.next(); ks = pr.ks.next(); vs = pr.vs.next(); gs = pr.gs.next()
            kb.dma("sp", [(qs[:], qT[j, :, :, t0:t0 + n].rearrange("c p t -> p c t"))], writes=[qs])
            kb.dma("sp", [(ks[:], kT[j, :, :, t0:t0 + n].rearrange("c p t -> p c t"))], writes=[ks])
            kb.dma("sp", [(vs[:], v[j, t0:t0 + n, :].rearrange("(t p) d -> p t d", p=128))], writes=[vs])
            kb.dma("sp", [(gs[:], gd[j, :, t0:t0 + n])], writes=[gs])
            cur.append((qs, ks, vs, gs))
        for tk in range(SEGT):
            gt = seg * SEGT + tk
            for pr, (qs, ks, vs, gs) in zip(prs, cur):
                tile_step(pr, gt, tk, qs, ks, vs, gs)
    print("B1 ninst", kb.ninst, "nwaits", kb.nwaits)
    return P.finish()

def pair_of(core, j):
    b = core // 4; idx = (core % 4) * 3 + j
    return b, idx // 2, idx % 2

def b1_maps(a1o, inp):
    def seqcat(b, name, axis):
        lat = np.concatenate([np.take(a1o[b * 4 + tt][name], np.arange(0, 2048), axis=axis) for tt in range(4)], axis=axis)
        ctx = np.take(a1o[b * 4][name], np.arange(2048, 2304), axis=axis)
        return lat, ctx
    full = {}
    for b in range(2):
        for name, axis in (("qT", 2), ("kT", 2), ("v", 0), ("gdT", 1)):
            full[(b, name)] = seqcat(b, name, axis)
    def seq(b, name, axis, d, sel):
        lat, ctx = full[(b, name)]
        lat = sel(lat); ctx = sel(ctx)
        if d == 1:
            lat = np.flip(lat, axis=axis); ctx = np.flip(ctx, axis=axis)
        return np.concatenate([ctx, lat], axis=axis)
    maps = []
    for core in range(8):
        m = {k: [] for k in ("qT", "kT", "v", "gd", "wg", "bg")}
        for j in range(3):
            b, h, d = pair_of(core, j)
            m["qT"].append(seq(b, "qT", 2, d, lambda a: a[2 * h:2 * h + 2]))
            m["kT"].append(seq(b, "kT", 2, d, lambda a: a[2 * h:2 * h + 2]))
            m["v"].append(seq(b, "v", 0, d, lambda a: a[:, h * 512:(h + 1) * 512]))
            m["gd"].append(seq(b, "gdT", 1, d, lambda a: a[16 * d:16 * d + 16]))
            w = inp["gla_wg_b" if d else "gla_wg_f"][0][:, h * 256:(h + 1) * 256]
            bb = inp["gla_bg_b" if d else "gla_bg_f"][0][h * 256:(h + 1) * 256]
            m["wg"].append(w); m["bg"].append(bb.reshape(2, 128).T)
        maps.append({k: np.ascontiguousarray(np.stack(vv)) for k, vv in m.items()})
    return maps
def build_c1():
    P = Prog(); kb = P.kb
    NL = 2048; NLT = 16
    of = P.din("of", [NL, 3072]); ob = P.din("ob", [NL, 3072])
    on_g = P.din("on_g", [1, 512]); sgl = P.din("sgl", [NL, 3072], BF16); sgfT = P.din("sgfT", [8, 128, NL], BF16)
    UCS = P.din("UCS", [8192, 4, 512], BF16); tabP = P.din("tabP", [2, 8192, NL], BF16)
    w_out = P.din("w_out", [4096, 4096]); x_in = P.din("x_in", [NL, 4096]); gmod = P.din("gmod", [1, 4096]); final_g = P.din("final_g", [1, 4096])
    out, out_d = P.dout("out", [NL, 4096])
    yT = P.nc.dram_tensor("yT_scr", [32, 128, NL], BF16, kind="Internal").ap(); yT_d = Dep("yT")
    x2 = P.nc.dram_tensor("x2_scr", [NL, 4096], F32, kind="Internal").ap(); x2_d = Dep("x2")
    identb, identf = make_ident(P, kb)
    psm = Rot([P.ps(f"psm{i}", [128, 512], F32) for i in range(6)])
    pst = Rot([P.ps(f"pst{i}", [128, 512], BF16) for i in range(2)])
    evq = Rot(["act", "dve"])
    with contextlib.ExitStack() as sc:
        tab = [P.sb(f"tab{i}", [128, 64, 512], BF16, sc) for i in range(2)]
        ub = Rot([P.sb(f"ub{i}", [128, 2, 64, 128], BF16, sc) for i in range(2)])
        sg = Rot([P.sb(f"sg{i}", [128, 512], BF16, sc) for i in range(2)])
        ys = Rot([P.sb(f"ys{i}", [128, 512], BF16, sc) for i in range(2)])
        for pg in range(4):
            for s_ in range(2):
                src = tabP[s_, :, pg * 512:(pg + 1) * 512].rearrange("(c p) n -> p c n", p=128)
                kb.dma("sp", [(tab[s_][:, 16 * i:16 * i + 16, :], src[:, 16 * i:16 * i + 16, :]) for i in range(4)], writes=[tab[s_]])
            for fb in range(8):
                u = ub.next(); grp = fb // 2; c0 = (fb % 2) * 128
                kb.dma("sp", [(u[:, s_, 32 * i:32 * i + 32, :], UCS[:, grp, s_ * 256 + c0:s_ * 256 + c0 + 128].rearrange("(c p) n -> p c n", p=128)[:, 32 * i:32 * i + 32, :])
                              for s_ in range(2) for i in range(2)], writes=[u])
                g_ = sg.next()
                kb.dma("sp", [(g_[:], sgfT[fb, :, pg * 512:(pg + 1) * 512])], writes=[g_])
                p = psm.next()
                for s_ in range(2):
                    for c in range(64):
                        first = (s_ == 0 and c == 0); last = (s_ == 1 and c == 63)
                        kb.op("pe", lambda e, s_=s_, c=c, u=u, p=p, first=first, last=last: e.matmul(p[:, :], u[:, s_, c, :], tab[s_][:, c, :], start=first, stop=last),
                              reads=[u, tab[s_]], writes=[p], accumulate=(not first))
                y_ = ys.next()
                kb.op("dve", lambda e, y_=y_, p=p, g_=g_: e.tensor_tensor(out=y_[:], in0=p[:, :], in1=g_[:], op=ALU.mult), reads=[p, g_], writes=[y_])
                kb.dma("sp", [(yT[24 + fb, :, pg * 512:(pg + 1) * 512], y_[:])], reads=[y_], writes=[yT_d])
        kb.barrier()
    with contextlib.ExitStack() as sc:
        gb = P.sb("ong", [128, 512], F32, sc)
        kb.dma("sp", [(gb[:], on_g.partition_broadcast(128))], writes=[gb])
        oa = Rot([P.sb(f"oa{i}", [128, 3072], F32, sc) for i in range(2)])
        obt = Rot([P.sb(f"obt{i}", [128, 3072], F32, sc) for i in range(2)])
        sgt = Rot([P.sb(f"sgt{i}", [128, 3072], BF16, sc) for i in range(2)])
        sq = P.sb("sq", [128, 3072], F32, sc); st = P.sb("st", [128, 16], F32, sc)
        yb = Rot([P.sb(f"ybf{i}", [128, 3072], BF16, sc) for i in range(2)])
        trs = Rot([P.sb(f"trs{i}", [128, 24, 128], BF16, sc) for i in range(2)])
        for ti in range(NLT):
            a = oa.next(); b_ = obt.next(); g_ = sgt.next()
            rs = slice(ti * 128, (ti + 1) * 128)
            kb.dma("sp", [(a[:, 1024 * i:1024 * (i + 1)], of[rs, 1024 * i:1024 * (i + 1)]) for i in range(3)], writes=[a])
            kb.dma("sp", [(b_[:, 1024 * i:1024 * (i + 1)], ob[rs, 1024 * i:1024 * (i + 1)]) for i in range(3)], writes=[b_])
            kb.dma("sp", [(g_[:], sgl[rs, :])], writes=[g_])
            kb.op("dve", lambda e, a=a, b_=b_: e.tensor_tensor(out=a[:], in0=a[:], in1=b_[:], op=ALU.add), reads=[a, b_], writes=[a])
            kb.op("pool", lambda e, a=a: e.tensor_tensor(out=sq[:], in0=a[:], in1=a[:], op=ALU.mult), reads=[a], writes=[sq])
            kb.op("dve", lambda e: e.tensor_reduce(out=st[:, 0:6], in_=sq[:].rearrange("p (h d) -> p h d", h=6), axis=AX.X, op=ALU.add), reads=[sq], writes=[st])
            kb.op("act", lambda e: e.activation(out=st[:, 8:14], in_=st[:, 0:6], func=AF.Sqrt, scale=1.0 / 512, bias=EPS), reads=[st], writes=[st])
            kb.op("dve", lambda e: e.reciprocal(out=st[:, 8:14], in_=st[:, 8:14]), reads=[st], writes=[st])
            a3 = a[:].rearrange("p (h d) -> p h d", h=6)
            kb.op("dve", lambda e, a3=a3: e.tensor_tensor(out=a3, in0=a3, in1=st[:, 8:14].unsqueeze(2).to_broadcast([128, 6, 512]), op=ALU.mult), reads=[a, st], writes=[a])
            kb.op("pool", lambda e, a3=a3: e.tensor_tensor(out=a3, in0=a3, in1=gb[:].unsqueeze(1).to_broadcast([128, 6, 512]), op=ALU.mult), reads=[a, gb], writes=[a])
            y_ = yb.next()
            kb.op("dve", lambda e, a=a, g_=g_, y_=y_: e.tensor_tensor(out=y_[:], in0=a[:], in1=g_[:], op=ALU.mult), reads=[a, g_], writes=[y_])
            s_ = trs.next()
            for q4 in range(6):
                p = pst.next()
                for jj in range(4):
                    blk = q4 * 4 + jj
                    kb.op("pe", lambda e, p=p, jj=jj, blk=blk, y_=y_: e.transpose(p[:, jj * 128:(jj + 1) * 128], y_[:, blk * 128:(blk + 1) * 128], identb[:]), reads=[y_, identb], writes=[p], accumulate=(jj > 0))
                eng = evq.next()
                o_ = s_[:, q4 * 4:q4 * 4 + 4, :]
                i_ = p[:, :].rearrange("p (b t) -> p b t", b=4)
                if eng == "act":
                    kb.op("act", lambda e, o_=o_, i_=i_: e.activation(out=o_, in_=i_, func=AF.Copy), reads=[p], writes=[s_])
                else:
                    kb.op("dve", lambda e, o_=o_, i_=i_: e.tensor_copy(out=o_, in_=i_), reads=[p], writes=[s_])
            kb.dma("sp", [(yT[0:24, :, rs].rearrange("b p t -> p b t"), s_[:])], reads=[s_], writes=[yT_d])
        kb.barrier()
    def yT_src(b0, b1, tok0, n):
        return yT[b0:b1, :, tok0:tok0 + n].rearrange("b p t -> p b t")
    kb.wait_all("sp", [yT_d])
    outproj_residual(P, kb, yT_src, w_out, x_in, gmod, x2, x2_d, psm, lambda gt: 0, GT=8, ntiles=NLT, ngm=1)
    kb.wait_all("sp", [x2_d])
    with contextlib.ExitStack() as sc:
        fg = P.sb("fg", [128, 4096], F32, sc)
        kb.dma("sp", [(fg[:], final_g.partition_broadcast(128))], writes=[fg])
        xt = Rot([P.sb(f"xf{i}", [128, 4096], F32, sc) for i in range(2)])
        yt = Rot([P.sb(f"yf{i}", [128, 4096], F32, sc) for i in range(2)])
        ss = P.sb("ssf", [128, 2], F32, sc)
        for ti in range(NLT):
            rs = slice(ti * 128, (ti + 1) * 128)
            x = xt.next(); y = yt.next()
            kb.dma("sp", [(x[:, 1024 * i:1024 * (i + 1)], x2[rs, 1024 * i:1024 * (i + 1)]) for i in range(4)], reads=[x2_d], writes=[x], track=x.d)
            kb.op("act", lambda e, x=x, y=y: e.activation(out=y[:], in_=x[:], func=AF.Square, accum_out=ss[:, 0:1]), reads=[x], writes=[y, ss])
            kb.op("act", lambda e: e.activation(out=ss[:, 1:2], in_=ss[:, 0:1], func=AF.Sqrt, scale=1.0 / 4096, bias=EPS), reads=[ss], writes=[ss])
            kb.op("dve", lambda e: e.reciprocal(out=ss[:, 1:2], in_=ss[:, 1:2]), reads=[ss], writes=[ss])
            kb.op("dve", lambda e, x=x, y=y: e.scalar_tensor_tensor(out=y[:], in0=x[:], scalar=ss[:, 1:2], in1=fg[:], op0=ALU.mult, op1=ALU.mult), reads=[x, ss, fg], writes=[y])
            kb.dma("sp", [(out[rs, 1024 * i:1024 * (i + 1)], y[:, 1024 * i:1024 * (i + 1)]) for i in range(4)], reads=[y], writes=[out_d])
    print("C1 ninst", kb.ninst, "nwaits", kb.nwaits)
    return P.finish()

def dft_p_table(t):
    p = np.arange(8192, dtype=np.int64)[:, None]; pp = np.arange(t * 2048, (t + 1) * 2048, dtype=np.int64)[None, :]
    ang = (2 * np.pi / 8192.0) * ((p * pp) % 8192)
    sc = 1.0 / np.sqrt(8192.0 * 256.0)
    return np.stack([(sc * np.cos(ang)).astype(NPBF), (-sc * np.sin(ang)).astype(NPBF)])

def c1_maps(a1o, b1o, x1o, mod, inp):
    ofull = np.zeros((2, 2, 8192, 3072), np.float32)
    for core in range(8):
        for j in range(3):
            b, h, d = pair_of(core, j)
            oo = b1o[core]["o"][j]
            ofull[b, d, :, h * 512:(h + 1) * 512] = oo[::-1] if d else oo
    maps = []
    for core in range(8):
        b, t = core // 4, core % 4
        rs = slice(t * 2048, (t + 1) * 2048)
        UCS = np.concatenate([a1o[b * 4 + tt]["ucs"][:2048] for tt in range(4)], axis=0)
        maps.append({"of": np.ascontiguousarray(ofull[b, 0, rs]), "ob": np.ascontiguousarray(ofull[b, 1, rs]), "on_g": inp["gla_on_g"][0][None],
                     "sgl": np.ascontiguousarray(a1o[core]["sgl"][:2048]), "sgfT": np.ascontiguousarray(a1o[core]["sgfT"][:, :, :2048]),
                     "UCS": np.ascontiguousarray(UCS), "tabP": dft_p_table(t), "w_out": inp["rec_w_out"][0],
                     "x_in": np.ascontiguousarray(x1o[core][:2048]), "gmod": np.ascontiguousarray(mod[b, 1, 8192:][None]), "final_g": inp["final_g"][None]})
    return maps
def _run(nc, maps):
    res = run_bass_kernel_spmd(nc, maps, core_ids=list(range(8)))
    return [{k: np.asarray(v) for k, v in r.items()} for r in res.results]

def kernel(**inp):
    inp = {k: np.asarray(v) for k, v in inp.items()}
    mod = run_mod(inp["c"], inp["c_ctx"], inp["ada_w"], inp["ada_b"])
    a0o = _run(build_a0(), a0_maps(inp["x"], inp["ctx"], mod, inp))
    b0o = _run(build_b0(), b0_maps(a0o, inp["x"], inp["ctx"], mod, inp))
    del a0o
    x1o = [o["x1"] for o in b0o]
    a1o = _run(build_a1(), a1_maps(x1o, mod, inp))
    b1o = _run(build_b1(), b1_maps(a1o, inp))
    c1o = _run(build_c1(), c1_maps(a1o, b1o, x1o, mod, inp))
    out = np.stack([np.concatenate([c1o[b * 4 + t]["out"] for t in range(4)], 0) for b in range(2)])
    return out.astype(np.float32)
```

```python
import contextlib
import numpy as np
import ml_dtypes
import concourse.bass as bass
import concourse.mybir as mybir
from concourse.bass_utils import run_bass_kernel_spmd

F32 = mybir.dt.float32; BF16 = mybir.dt.bfloat16
AF = mybir.ActivationFunctionType; ALU = mybir.AluOpType; AX = mybir.AxisListType
NPBF = ml_dtypes.bfloat16
EPS = 1e-6


class Dep:
    __slots__ = ("name", "lw", "rd", "dsem")
    def __init__(self, name=""):
        self.name = name; self.lw = None; self.rd = {}; self.dsem = None


class T:
    __slots__ = ("t", "d")
    def __init__(self, t, name):
        self.t = t; self.d = Dep(name)
    def __getitem__(self, k):
        return self.t[k]


class KB:
    ENG = ("pe", "act", "dve", "pool", "sp")
    def __init__(self, nc, stack):
        self.nc = nc; self.stack = stack
        self.ops = {e: [] for e in self.ENG}
        self.sem = {e: stack.enter_context(nc.semaphore("s_" + e)) for e in ("pe", "act", "dve", "pool")}
        self.allsems = {id(s): s for s in self.sem.values()}
        self.cnt = {}
        self.waited = {e: {} for e in self.ENG}
        self.nwaits = 0; self.ninst = 0
        self.sem_pool = []
    def newsem(self, name):
        if self.sem_pool:
            return self.sem_pool.pop()
        s = self.stack.enter_context(self.nc.semaphore(name))
        self.allsems[id(s)] = s
        return s
    def _need(self, eng, reads, writes, accumulate=False):
        need = {}
        def add(p):
            if p is None: return
            s, v = p
            k = id(s)
            if k not in need or need[k][1] < v: need[k] = (s, v)
        for r in reads:
            add(r.lw)
        for w in writes:
            if not accumulate: add(w.lw)
            for p in w.rd.values(): add(p)
        wd = self.waited[eng]
        for k, (s, v) in need.items():
            if wd.get(k, 0) < v:
                wd[k] = v
                self.ops[eng].append(lambda e, s=s, v=v: e.wait_ge(s, v))
                self.nwaits += 1
    def op(self, eng, fn, reads=(), writes=(), accumulate=False):
        reads = [r.d if isinstance(r, T) else r for r in reads]
        writes = [w.d if isinstance(w, T) else w for w in writes]
        self._need(eng, reads, writes, accumulate)
        s = self.sem[eng]; k = id(s)
        c = self.cnt.get(k, 0) + 1; self.cnt[k] = c
        self.ops[eng].append(lambda e, fn=fn, s=s: fn(e).then_inc(s, 1))
        self.ninst += 1
        for r in reads: r.rd[k] = (s, c)
        for w in writes:
            w.lw = (s, c); w.rd = {}
    def dma(self, q, pieces, reads=(), writes=(), track=None):
        reads = [r.d if isinstance(r, T) else r for r in reads]
        writes = [w.d if isinstance(w, T) else w for w in writes]
        self._need(q, reads, writes)
        t = track if track is not None else (writes[0] if writes else reads[0])
        if t.dsem is None: t.dsem = self.newsem("d_" + t.name)
        s = t.dsem; k = id(s)
        c = self.cnt.get(k, 0)
        for (o, i) in pieces:
            c += 16
            self.ops[q].append(lambda e, o=o, i=i, s=s: e.dma_start(out=o, in_=i).then_inc(s, 16))
            self.ninst += 1
        self.cnt[k] = c
        for r in reads: r.rd[k] = (s, c)
        for w in writes:
            w.lw = (s, c); w.rd = {}
    def coll(self, kind, op, groups, i_ap, i_dep, o_ap, o_dep):
        self._need("pool", [i_dep], [o_dep])
        if o_dep.dsem is None: o_dep.dsem = self.newsem("c_" + o_dep.name)
        s = o_dep.dsem; k = id(s)
        c = self.cnt.get(k, 0) + 1; self.cnt[k] = c
        self.ops["pool"].append(lambda e: e.collective_compute(kind, op, replica_groups=groups, ins=[i_ap], outs=[o_ap]).then_inc(s, 1))
        self.ninst += 1
        i_dep.rd[k] = (s, c); o_dep.lw = (s, c); o_dep.rd = {}
    def wait_all(self, eng, deps):
        deps = [r.d if isinstance(r, T) else r for r in deps]
        self._need(eng, deps, deps)
    def barrier(self):
        for eng in self.ENG:
            wd = self.waited[eng]
            for k, s in self.allsems.items():
                v = self.cnt.get(k, 0)
                if v > 0 and wd.get(k, 0) < v:
                    wd[k] = v
                    self.ops[eng].append(lambda e, s=s, v=v: e.wait_ge(s, v))
    def emit(self, block):
        m = {"pe": block.tensor, "act": block.scalar, "dve": block.vector, "pool": block.gpsimd, "sp": block.sync}
        for en, dec in m.items():
            lst = self.ops[en]
            def f(e, lst=lst):
                for g in lst: g(e)
            dec(f)


class ScopeStack(contextlib.ExitStack):
    def __init__(self, kb):
        super().__init__(); self.kb = kb; self.deps = []
    def __exit__(self, *a):
        for d in self.deps:
            if d.dsem is not None:
                self.kb.sem_pool.append(d.dsem); d.dsem = None
        return super().__exit__(*a)


class Prog:
    def __init__(self):
        self.nc = bass.Bass("TRN2", target_bir_lowering=False, num_devices=8)
        self.st = contextlib.ExitStack()
        self.kb = KB(self.nc, self.st)
        self.outs = []
        self.n = 0
    def din(self, name, shape, dt=F32):
        return self.nc.dram_tensor(name, list(shape), dt, kind="ExternalInput").ap()
    def dout(self, name, shape, dt=F32):
        ap = self.nc.dram_tensor(name, list(shape), dt, kind="ExternalOutput").ap()
        d = Dep(name); self.outs.append(d)
        return ap, d
    def sb(self, name, shape, dt, stack=None):
        st = stack or self.st
        self.n += 1; name = f"{name}_{self.n}"
        t = T(st.enter_context(self.nc.sbuf_tensor(name, list(shape), dt)), name)
        if isinstance(st, ScopeStack): st.deps.append(t.d)
        return t
    def ps(self, name, shape, dt, stack=None):
        st = stack or self.st
        self.n += 1; name = f"{name}_{self.n}"
        return T(st.enter_context(self.nc.psum_tensor(name, list(shape), dt)), name)
    def scr(self, name, shape, dt=F32):
        return self.nc.dram_tensor(name, list(shape), dt, kind="Internal").ap(), Dep(name)
    def finish(self):
        self.kb.wait_all("sp", self.outs)
        with self.nc.Block() as block:
            self.kb.emit(block)
        self.st.close()
        return self.nc


def make_ident(P, kb, name="ident"):
    idf = P.sb(name + "f", [128, 128], F32)
    idb = P.sb(name, [128, 128], BF16)
    kb.op("pool", lambda e: e.memset(idf[:], 1.0), writes=[idf])
    kb.op("pool", lambda e: e.affine_select(out=idf[:], in_=idf[:], pattern=[[-1, 128]], compare_op=ALU.is_equal,
                                            fill=0.0, base=0, channel_multiplier=1), reads=[idf], writes=[idf])
    kb.op("dve", lambda e: e.tensor_copy(out=idb[:], in_=idf[:]), reads=[idf], writes=[idb])
    return idb, idf
D = 4096
NTOK = 2304; NT = 18
G4 = [[0, 1, 2, 3], [4, 5, 6, 7]]
G8 = [[0, 1, 2, 3, 4, 5, 6, 7]]

class Rot:
    def __init__(self, items): self.items = items; self.i = 0
    def next(self):
        x = self.items[self.i % len(self.items)]; self.i += 1
        return x

def load_w(kb, wt, src, kch, pieces=4):
    ncols = src.shape[1]
    s3 = src.rearrange("(c p) n -> p c n", p=128)
    pieces = min(pieces, kch)
    step = kch // pieces
    wv = wt[:, 0:kch * ncols].rearrange("p (c n) -> p c n", c=kch)
    kb.dma("pool", [(wv[:, i * step:(i + 1) * step, :], s3[:, i * step:(i + 1) * step, :]) for i in range(pieces)], writes=[wt])
    r = T(wv, "wv"); r.d = wt.d
    return r

def emit_mod(P, kb, cT, aw, ab, mod_all, mod_all_d, ncols=3072):
    msend, msend_d = P.scr("mod_send", [3, ncols])
    with ScopeStack(kb) as sc:
        ct = P.sb("ct", [128, 32, 3], F32, sc); scl = P.sb("sc", [128, 32, 3], F32, sc)
        wb = [P.sb(f"wb{i}", [128, 32, 512], F32, sc) for i in range(2)]
        bt = P.sb("bt", [3, ncols], F32, sc); ot = P.sb("ot", [3, ncols], F32, sc)
        ps = [P.ps(f"psmod{i}", [128, 512], F32, sc) for i in range(2)]
        kb.dma("sp", [(ct[:], cT.rearrange("(c p) r -> p c r", p=128))], writes=[ct])
        kb.dma("sp", [(bt[:], ab.partition_broadcast(3))], writes=[bt])
        kb.op("act", lambda e: e.activation(out=scl[:], in_=ct[:], func=AF.Silu), reads=[ct], writes=[scl])
        for j in range(ncols // 512):
            w = wb[j % 2]
            src = aw[:, j * 512:(j + 1) * 512].rearrange("(c p) n -> p c n", p=128)
            kb.dma("sp", [(w[:, 8 * i:8 * i + 8, :], src[:, 8 * i:8 * i + 8, :]) for i in range(4)], writes=[w])
            p = ps[j % 2]
            for kc in range(32):
                kb.op("pe", lambda e, p=p, w=w, kc=kc: e.matmul(p[0:3, :], scl[:, kc, :], w[:, kc, :], start=(kc == 0), stop=(kc == 31)),
                      reads=[scl, w], writes=[p], accumulate=(kc > 0))
            kb.op("dve", lambda e, p=p, j=j: e.tensor_tensor(out=ot[:, j * 512:(j + 1) * 512], in0=p[0:3, :], in1=bt[:, j * 512:(j + 1) * 512], op=ALU.add),
                  reads=[p, bt], writes=[ot])
        kb.dma("sp", [(msend[:, :], ot[:])], reads=[ot], writes=[msend_d])
        kb.coll("AllGather", ALU.bypass, G8, msend[:, :], msend_d, mod_all[:, :], mod_all_d)
        kb.barrier()

def mod_runs(l, part):
    runs = []
    c = 0
    while c < 32:
        cc = l * 96 + part * 32 + c
        rank, w = cc // 24, cc % 24
        n = min(32 - c, 24 - w)
        runs.append((rank, w, n, c)); c += n
    return runs

def build_mod_tiles(P, kb, mod_all, l, ngT, bsel, identf, psum_t, G, Sh):
    with ScopeStack(kb) as sc:
        tin = [P.sb(f"tin{r}", [64, 128], F32, sc) for r in range(3)]
        mt = [P.sb(f"mt{r}", [128, 64], F32, sc) for r in range(3)]
        for r in range(3):
            pieces = []
            for part in range(2):
                for (rank, w0, n, c0) in mod_runs(l, part):
                    pieces.append((tin[r][part * 32 + c0:part * 32 + c0 + n, :], mod_all[rank * 3 + r, w0 * 128:(w0 + n) * 128].rearrange("(c p) -> c p", p=128)))
            kb.dma("sp", pieces, writes=[tin[r]])
            kb.op("pe", lambda e, r=r: e.transpose(psum_t[:, 0:64], tin[r][0:64, :], identf[0:64, 0:64]), reads=[tin[r], identf], writes=[psum_t])
            kb.op("dve", lambda e, r=r: e.tensor_copy(out=mt[r][:], in_=psum_t[:, 0:64]), reads=[psum_t], writes=[mt[r]])
        kb.op("dve", lambda e: e.tensor_scalar(out=mt[0][:], in0=mt[0][:], scalar1=bsel[:, 0:1], scalar2=None, op0=ALU.mult), reads=[mt[0], bsel], writes=[mt[0]])
        kb.op("dve", lambda e: e.scalar_tensor_tensor(out=mt[0][:], in0=mt[1][:], scalar=bsel[:, 1:2], in1=mt[0][:], op0=ALU.mult, op1=ALU.add), reads=[mt[0], mt[1], bsel], writes=[mt[0]])
        for sel, m in ((0, mt[0]), (1, mt[2])):
            kb.op("dve", lambda e, sel=sel, m=m: e.tensor_scalar(out=G[:, sel, :], in0=m[:, 32:64], scalar1=1.0, scalar2=None, op0=ALU.add), reads=[m], writes=[G])
            kb.op("dve", lambda e, sel=sel: e.tensor_tensor(out=G[:, sel, :], in0=G[:, sel, :], in1=ngT[:, l, :], op=ALU.mult), reads=[ngT, G], writes=[G])
            kb.op("dve", lambda e, sel=sel, m=m: e.tensor_copy(out=Sh[:, sel, :], in_=m[:, 0:32]), reads=[m], writes=[Sh])
        kb.barrier()

def load_gate_bcast(P, kb, mod_all, l, bsel, gm, tmp, nsel):
    def row_pieces(dst, r):
        return [(dst[:, c0 * 128:(c0 + n) * 128], mod_all[rank * 3 + r:rank * 3 + r + 1, w0 * 128:(w0 + n) * 128].partition_broadcast(128)) for (rank, w0, n, c0) in mod_runs(l, 2)]
    kb.dma("sp", row_pieces(gm[:, 0, :], 0), writes=[gm])
    kb.dma("sp", row_pieces(tmp[:, :], 1), writes=[tmp])
    kb.op("dve", lambda e: e.tensor_scalar(out=gm[:, 0, :], in0=gm[:, 0, :], scalar1=bsel[:, 0:1], scalar2=None, op0=ALU.mult), reads=[gm, bsel], writes=[gm])
    kb.op("dve", lambda e: e.scalar_tensor_tensor(out=gm[:, 0, :], in0=tmp[:, :], scalar=bsel[:, 1:2], in1=gm[:, 0, :], op0=ALU.mult, op1=ALU.add), reads=[gm, tmp, bsel], writes=[gm])
    if nsel == 2:
        kb.dma("sp", row_pieces(gm[:, 1, :], 2), writes=[gm])
def rms_prep(P, kb, GT, g0, x_dram, modT, col_lat, hT, psr, identf, scope, G, Sh, evq):
    xt = [P.sb(f"xt{i}", [128, 4096], F32, scope) for i in range(2)]
    yt = P.sb("yt", [128, 4096], F32, scope)
    ss = P.sb("ss", [128, 2], F32, scope)
    for ti in range(GT):
        gt = g0 + ti
        x = xt[ti % 2]
        kb.dma("sp", [(x[:, 1024 * i:1024 * (i + 1)], x_dram[gt * 128:(gt + 1) * 128, 1024 * i:1024 * (i + 1)]) for i in range(4)], writes=[x])
        kb.op("act", lambda e, x=x: e.activation(out=yt[:], in_=x[:], func=AF.Square, accum_out=ss[:, 0:1]), reads=[x], writes=[yt, ss])
        kb.op("act", lambda e: e.activation(out=ss[:, 1:2], in_=ss[:, 0:1], func=AF.Sqrt, scale=1.0 / 4096, bias=EPS), reads=[ss], writes=[ss])
        kb.op("dve", lambda e: e.reciprocal(out=ss[:, 1:2], in_=ss[:, 1:2]), reads=[ss], writes=[ss])
        kb.op("dve", lambda e, x=x: e.tensor_scalar(out=yt[:], in0=x[:], scalar1=ss[:, 1:2], scalar2=None, op0=ALU.mult), reads=[x, ss], writes=[yt])
        sel = 0 if col_lat(gt) else 1
        for c4 in range(8):
            p = psr.next()
            for j in range(4):
                c = c4 * 4 + j
                kb.op("pe", lambda e, p=p, j=j, c=c: e.transpose(p[:, j * 128:(j + 1) * 128], yt[:, c * 128:(c + 1) * 128], identf[:]),
                      reads=[yt, identf], writes=[p], accumulate=(j > 0))
            for j in range(4):
                c = c4 * 4 + j
                eng = evq.next()
                o = hT[:, c, ti * 128:(ti + 1) * 128]
                if eng == "act":
                    kb.op("act", lambda e, o=o, p=p, j=j, c=c, sel=sel: e.activation(out=o, in_=p[:, j * 128:(j + 1) * 128], func=AF.Identity,
                          scale=G[:, sel, c:c + 1], bias=Sh[:, sel, c:c + 1]), reads=[p, G, Sh], writes=[hT])
                else:
                    kb.op("dve", lambda e, o=o, p=p, j=j, c=c, sel=sel: e.tensor_scalar(out=o, in0=p[:, j * 128:(j + 1) * 128],
                          scalar1=G[:, sel, c:c + 1], scalar2=Sh[:, sel, c:c + 1], op0=ALU.mult, op1=ALU.add), reads=[p, G, Sh], writes=[hT])

def head_norm_rope(P, kb, pm, nh, hd, gain_b, rope_t, ti, out_bf, tmp, do_norm=True, do_rope=True):
    half = hd // 2
    v3 = lambda t: t[:, 0:nh * hd].rearrange("p (h d) -> p h d", h=nh)
    if do_norm:
        sq, st = tmp["sq"], tmp["st"]
        kb.op("pool", lambda e: e.tensor_tensor(out=sq[:, 0:nh * hd], in0=pm[:, 0:nh * hd], in1=pm[:, 0:nh * hd], op=ALU.mult), reads=[pm], writes=[sq])
        kb.op("dve", lambda e: e.tensor_reduce(out=st[:, 0:nh], in_=v3(sq), axis=AX.X, op=ALU.add), reads=[sq], writes=[st])
        kb.op("act", lambda e: e.activation(out=st[:, 16:16 + nh], in_=st[:, 0:nh], func=AF.Sqrt, scale=1.0 / hd, bias=EPS), reads=[st], writes=[st])
        kb.op("dve", lambda e: e.reciprocal(out=st[:, 16:16 + nh], in_=st[:, 16:16 + nh]), reads=[st], writes=[st])
        kb.op("dve", lambda e: e.tensor_tensor(out=v3(pm), in0=v3(pm), in1=st[:, 16:16 + nh].unsqueeze(2).to_broadcast([128, nh, hd]), op=ALU.mult),
              reads=[pm, st], writes=[pm])
        kb.op("pool", lambda e: e.tensor_tensor(out=v3(pm), in0=v3(pm), in1=gain_b[:, 0:hd].unsqueeze(1).to_broadcast([128, nh, hd]), op=ALU.mult),
              reads=[pm, gain_b], writes=[pm])
    if not do_rope:
        kb.op("dve", lambda e: e.tensor_copy(out=out_bf, in_=pm[:, 0:nh * hd]), reads=[pm], writes=[tmp["outd"]])
        return
    t1, t2 = tmp["t1"], tmp["t2"]
    cosb = rope_t[:, ti, 0:half].unsqueeze(1).to_broadcast([128, nh, half])
    sinb = rope_t[:, ti, half:hd].unsqueeze(1).to_broadcast([128, nh, half])
    x1 = v3(pm)[:, :, 0:half]; x2 = v3(pm)[:, :, half:hd]
    o3 = out_bf.rearrange("p (h d) -> p h d", h=nh)
    t1a = t1[:, 0:nh * half].rearrange("p (h d) -> p h d", h=nh); t1b = t1[:, nh * half:2 * nh * half].rearrange("p (h d) -> p h d", h=nh)
    t2a = t2[:, 0:nh * half].rearrange("p (h d) -> p h d", h=nh); t2b = t2[:, nh * half:2 * nh * half].rearrange("p (h d) -> p h d", h=nh)
    kb.op("dve", lambda e: e.tensor_tensor(out=t1a, in0=x1, in1=cosb, op=ALU.mult), reads=[pm, rope_t], writes=[t1])
    kb.op("pool", lambda e: e.tensor_tensor(out=t2a, in0=x2, in1=sinb, op=ALU.mult), reads=[pm, rope_t], writes=[t2])
    kb.op("dve", lambda e: e.tensor_tensor(out=t1b, in0=x2, in1=cosb, op=ALU.mult), reads=[pm, rope_t], writes=[t1])
    kb.op("pool", lambda e: e.tensor_tensor(out=t2b, in0=x1, in1=sinb, op=ALU.mult), reads=[pm, rope_t], writes=[t2])
    kb.op("dve", lambda e: e.tensor_tensor(out=o3[:, :, 0:half], in0=t1a, in1=t2a, op=ALU.subtract), reads=[t1, t2], writes=[tmp["outd"]])
    kb.op("pool", lambda e: e.tensor_tensor(out=o3[:, :, half:hd], in0=t1b, in1=t2b, op=ALU.add), reads=[t1, t2], writes=[tmp["outd"]])

def emit_a0(P, kb, io, G, Sh, identb, identf, GT=6):
    NG = NT // GT
    GTOK = GT * 128
    nchunks = [(i, min(512, GTOK - i)) for i in range(0, GTOK, 512)]
    xin, w_in, gains, w_uq, w_ukv, ropeA, ropeB = io["xin"], io["att_w_in"], io["gains"], io["w_uq"], io["w_ukv"], io["ropeA"], io["ropeB"]
    qaT, qaT_d = io["qaT"]; qbnT, qbnT_d = io["qbnT"]; qbrT, qbrT_d = io["qbrT"]; gT, gT_d = io["gT"]
    ktsend, kt_d = io["ktsend"]; ktctx, _ = io["ktctx"]; vsend, v_d = io["vsend"]; vctx, _ = io["vctx"]
    modT = None
    def kt_heads(c0, nch, gt):
        return ktsend[c0:c0 + nch, :, gt * 128:(gt + 1) * 128] if gt < 16 else ktctx[c0:c0 + nch, :, (gt - 16) * 128:(gt - 15) * 128]
    def v_dst(gt, ch):
        return vsend[ch, gt * 128:(gt + 1) * 128, :] if gt < 16 else vctx[ch, (gt - 16) * 128:(gt - 15) * 128, :]
    def kt_span(ch, tok0, s):
        pcs = []
        nlat = max(0, min(GTOK, 2048 - tok0))
        if nlat > 0: pcs.append((ktsend[ch, :, tok0:tok0 + nlat], s[:, 0:nlat]))
        if nlat < GTOK:
            c0 = max(tok0, 2048) - 2048
            pcs.append((ktctx[ch, :, c0:c0 + GTOK - nlat], s[:, nlat:GTOK]))
        return pcs
    with ScopeStack(kb) as sc0:
        gb = P.sb("gb", [128, 1792], F32, sc0)
        kb.dma("sp", [(gb[:], gains.partition_broadcast(128))], writes=[gb])
        gq = T(gb.t, "gq"); gq.d = gb.d
        hT = P.sb("hT", [128, 32, GTOK], BF16, sc0)
        cqT = P.sb("cqT", [128, 8, GTOK], BF16, sc0); ckvT = P.sb("ckvT", [128, 4, GTOK], BF16, sc0)
        psm = Rot([P.ps(f"psm{i}", [128, 512], F32, sc0) for i in range(6)])
        pst = Rot([P.ps(f"pst{i}", [128, 512], BF16, sc0) for i in range(2)])
        evq = Rot(["act", "dve"])
        col_lat = lambda gt: gt < 16
        for g in range(NG):
            g0 = g * GT; tok0 = g0 * 128
            with ScopeStack(kb) as sc1:
                rms_prep(P, kb, GT, g0, xin, modT, col_lat, hT, psm, identf, sc1, G, Sh, evq)
                kb.barrier()
            with ScopeStack(kb) as sc2:
                wbuf = Rot([P.sb(f"wbuf{i}", [128, 16384], BF16, sc2) for i in range(2)])
                pm = Rot([P.sb(f"pm{i}", [128, 1024], F32, sc2) for i in range(2)])
                tmp = {"sq": P.sb("sq", [128, 1024], F32, sc2), "st": P.sb("st", [128, 32], F32, sc2),
                       "t1": P.sb("t1", [128, 512], F32, sc2), "t2": P.sb("t2", [128, 512], F32, sc2)}
                rA = P.sb("rA", [128, GT, 128], F32, sc2); rB = P.sb("rB", [128, GT, 64], F32, sc2)
                kb.dma("sp", [(rA[:], ropeA[tok0:tok0 + GTOK, :].rearrange("(t p) d -> p t d", p=128))], writes=[rA])
                kb.dma("sp", [(rB[:], ropeB[tok0:tok0 + GTOK, :].rearrange("(t p) d -> p t d", p=128))], writes=[rB])
                tmb = Rot([P.sb(f"tmb{i}", [128, 1024], BF16, sc2) for i in range(2)])
                fms = Rot([P.sb(f"fms{i}", [128, GTOK], BF16, sc2) for i in range(3)])
                trs = Rot([P.sb(f"trs{i}", [128, 512], BF16, sc2) for i in range(3)])

                def tm_block(wt, ncols, ti, p):
                    for kc in range(32):
                        kb.op("pe", lambda e, kc=kc: e.matmul(p[:, 0:ncols], hT[:, kc, ti * 128:(ti + 1) * 128], wt[:, kc, 0:ncols], start=(kc == 0), stop=(kc == 31)),
                              reads=[hT, wt], writes=[p], accumulate=(kc > 0))

                def transpose_out(src_bf, nblk, dst_fn, dst_dep, src_dep, rows=128):
                    p = pst.next()
                    for j in range(nblk):
                        kb.op("pe", lambda e, j=j: e.transpose(p[:, j * 128:(j + 1) * 128], src_bf[:, j * 128:(j + 1) * 128], identb[:]),
                              reads=[src_dep, identb], writes=[p], accumulate=(j > 0))
                    s = trs.next()
                    eng = evq.next()
                    if eng == "act":
                        kb.op("act", lambda e: e.activation(out=s[:, 0:nblk * 128], in_=p[:, 0:nblk * 128], func=AF.Copy), reads=[p], writes=[s])
                    else:
                        kb.op("dve", lambda e: e.tensor_copy(out=s[:, 0:nblk * 128], in_=p[:, 0:nblk * 128]), reads=[p], writes=[s])
                    dst_fn(s)

                blocks = [("qa", 0, 512, 0), ("qa", 512, 512, 1), ("qa", 1024, 512, 2), ("qa", 1536, 512, 3), ("ka", 2048, 512, 0),
                          ("va", 2560, 512, 0), ("ckv", 4096, 512, 0), ("kr", 4608, 64, 0)]
                nxt = load_w(kb, wbuf.next(), w_in[:, blocks[0][1]:blocks[0][1] + blocks[0][2]], 32)
                for bi, (kind, c0, ncols, idx) in enumerate(blocks):
                    wt = nxt
                    if bi + 1 < len(blocks):
                        nxt = load_w(kb, wbuf.next(), w_in[:, blocks[bi + 1][1]:blocks[bi + 1][1] + blocks[bi + 1][2]], 32)
                    for ti in range(GT):
                        gt = g0 + ti
                        p = psm.next()
                        tm_block(wt, ncols, ti, p)
                        if kind in ("qa", "ka"):
                            m = pm.next()
                            kb.op("act", lambda e, m=m, p=p: e.activation(out=m[:, 0:512], in_=p[:, :], func=AF.Copy), reads=[p], writes=[m])
                            ob = tmb.next(); tmp["outd"] = ob.d
                            gain = gb[:, 0:128] if kind == "qa" else gb[:, 128:256]
                            gt_ = T(gain, "g"); gt_.d = gb.d
                            head_norm_rope(P, kb, m, 4, 128, gt_, rA, ti, ob[:, 0:512], tmp)
                            if kind == "qa":
                                dst = lambda s, idx=idx, gt=gt: kb.dma("sp", [(qaT[idx * 4:idx * 4 + 4, :, gt * 128:(gt + 1) * 128].rearrange("h d t -> d h t"),
                                                                             s[:, 0:512].rearrange("d (h t) -> d h t", h=4))], reads=[s], writes=[qaT_d])
                            else:
                                dst = lambda s, gt=gt: kb.dma("sp", [(kt_heads(0, 4, gt).rearrange("h d t -> d h t"),
                                                                      s[:, 0:512].rearrange("d (h t) -> d h t", h=4))], reads=[s], writes=[kt_d])
                            transpose_out(ob, 4, dst, None, ob)
                        elif kind == "va":
                            ob = tmb.next()
                            eng = evq.next()
                            if eng == "act":
                                kb.op("act", lambda e, ob=ob, p=p: e.activation(out=ob[:, 0:512], in_=p[:, :], func=AF.Copy), reads=[p], writes=[ob])
                            else:
                                kb.op("dve", lambda e, ob=ob, p=p: e.tensor_copy(out=ob[:, 0:512], in_=p[:, :]), reads=[p], writes=[ob])
                            kb.dma("sp", [(v_dst(gt, 0), ob[:, 0:256]), (v_dst(gt, 1), ob[:, 256:512])], reads=[ob], writes=[v_d])
                        elif kind == "ckv":
                            m = pm.next()
                            kb.op("act", lambda e, m=m, p=p: e.activation(out=m[:, 0:512], in_=p[:, :], func=AF.Copy), reads=[p], writes=[m])
                            ob = tmb.next(); tmp["outd"] = ob.d
                            gt_ = T(gb[:, 1280:1792], "g"); gt_.d = gb.d
                            head_norm_rope(P, kb, m, 1, 512, gt_, None, ti, ob[:, 0:512], tmp, do_rope=False)
                            def dst(s, ti=ti):
                                kb.op("pool", lambda e: e.tensor_copy(out=ckvT[:, :, ti * 128:(ti + 1) * 128], in_=s[:, 0:512].rearrange("d (c t) -> d c t", c=4)), reads=[s], writes=[ckvT])
                            transpose_out(ob, 4, dst, None, ob)
                        elif kind == "kr":
                            m = pm.next()
                            kb.op("act", lambda e, m=m, p=p: e.activation(out=m[:, 0:64], in_=p[:, 0:64], func=AF.Copy), reads=[p], writes=[m])
                            ob = tmb.next(); tmp["outd"] = ob.d
                            kb.op("pool", lambda e, ob=ob: e.memset(ob[:, 0:128], 0.0), writes=[ob])
                            head_norm_rope(P, kb, m, 1, 64, None, rB, ti, ob[:, 0:64], tmp, do_norm=False)
                            dst = lambda s, gt=gt: kb.dma("sp", [(kt_heads(20, 1, gt)[0, 0:64, :], s[0:64, 0:128])], reads=[s], writes=[kt_d])
                            transpose_out(ob, 1, dst, None, ob)
                wa = load_w(kb, wbuf.next(), w_in[:, 3072:3584], 32)
                wb_ = load_w(kb, wbuf.next(), w_in[:, 3584:4096], 32)
                for ti in range(GT):
                    m = pm.next()
                    for half, wt in enumerate((wa, wb_)):
                        p = psm.next()
                        tm_block(wt, 512, ti, p)
                        kb.op("act", lambda e, m=m, p=p, half=half: e.activation(out=m[:, half * 512:(half + 1) * 512], in_=p[:, :], func=AF.Copy), reads=[p], writes=[m])
                    ob = tmb.next(); tmp["outd"] = ob.d
                    gt_ = T(gb[:, 256:1280], "g"); gt_.d = gb.d
                    head_norm_rope(P, kb, m, 1, 1024, gt_, None, ti, ob[:, 0:1024], tmp, do_rope=False)
                    for hh in range(2):
                        def dst(s, ti=ti, hh=hh):
                            kb.op("pool", lambda e: e.tensor_copy(out=cqT[:, hh * 4:hh * 4 + 4, ti * 128:(ti + 1) * 128], in_=s[:, 0:512].rearrange("d (c t) -> d c t", c=4)), reads=[s], writes=[cqT])
                        obh = T(ob[:, hh * 512:(hh + 1) * 512], "obh"); obh.d = ob.d
                        transpose_out(obh, 4, dst, None, ob)
                nxt = load_w(kb, wbuf.next(), w_in[:, 4672:4672 + 512], 32)
                for wbi in range(8):
                    wt = nxt
                    if wbi + 1 < 8:
                        nxt = load_w(kb, wbuf.next(), w_in[:, 4672 + (wbi + 1) * 512:4672 + (wbi + 2) * 512], 32)
                    for sbk in range(4):
                        blk = wbi * 4 + sbk
                        s = fms.next()
                        for (n0, n) in nchunks:
                            p = psm.next()
                            for kc in range(32):
                                kb.op("pe", lambda e, kc=kc, p=p, n0=n0, n=n, sbk=sbk, wt=wt: e.matmul(p[:, 0:n], wt[:, kc, sbk * 128:(sbk + 1) * 128], hT[:, kc, n0:n0 + n], start=(kc == 0), stop=(kc == 31)),
                                      reads=[hT, wt], writes=[p], accumulate=(kc > 0))
                            kb.op("act", lambda e, p=p, s=s, n0=n0, n=n: e.activation(out=s[:, n0:n0 + n], in_=p[:, 0:n], func=AF.Silu), reads=[p], writes=[s])
                        kb.dma("sp", [(gT[blk, :, tok0:tok0 + GTOK], s[:, :])], reads=[s], writes=[gT_d])
                nxt = load_w(kb, wbuf.next(), w_uq[:, 0:768], 8)
                for hb in range(4):
                    wt = nxt
                    if hb + 1 < 4:
                        nxt = load_w(kb, wbuf.next(), w_uq[:, (hb + 1) * 768:(hb + 2) * 768], 8)
                    else:
                        nxt = load_w(kb, wbuf.next(), w_ukv[:, 0:512], 4)
                    for hh in range(4):
                        h = hb * 4 + hh
                        s = fms.next()
                        for (n0, n) in nchunks:
                            p = psm.next()
                            for kc in range(8):
                                kb.op("pe", lambda e, kc=kc, p=p, n0=n0, n=n, hh=hh, wt=wt: e.matmul(p[:, 0:n], wt[:, kc, hh * 192:hh * 192 + 128], cqT[:, kc, n0:n0 + n], start=(kc == 0), stop=(kc == 7)),
                                      reads=[cqT, wt], writes=[p], accumulate=(kc > 0))
                            eng = evq.next()
                            if eng == "act":
                                kb.op("act", lambda e, p=p, s=s, n0=n0, n=n: e.activation(out=s[:, n0:n0 + n], in_=p[:, 0:n], func=AF.Copy), reads=[p], writes=[s])
                            else:
                                kb.op("dve", lambda e, p=p, s=s, n0=n0, n=n: e.tensor_copy(out=s[:, n0:n0 + n], in_=p[:, 0:n]), reads=[p], writes=[s])
                        kb.dma("sp", [(qbnT[h, :, tok0:tok0 + GTOK], s[:, :])], reads=[s], writes=[qbnT_d])
                    for ti in range(GT):
                        gt = g0 + ti
                        p = psm.next()
                        for kc in range(8):
                            kb.op("pe", lambda e, kc=kc, p=p, ti=ti, wt=wt: e.matmul(p[:, 0:256], cqT[:, kc, ti * 128:(ti + 1) * 128],
                                  wt[:, kc, :].rearrange("p (h d) -> p h d", h=4)[:, :, 128:192], start=(kc == 0), stop=(kc == 7)),
                                  reads=[cqT, wt], writes=[p], accumulate=(kc > 0))
                        m = pm.next()
                        kb.op("act", lambda e, m=m, p=p: e.activation(out=m[:, 0:256], in_=p[:, 0:256], func=AF.Copy), reads=[p], writes=[m])
                        ob = tmb.next(); tmp["outd"] = ob.d
                        head_norm_rope(P, kb, m, 4, 64, None, rB, ti, ob[:, 0:256], tmp, do_norm=False)
                        dst = lambda s, hb=hb, gt=gt: kb.dma("sp", [(qbrT[hb * 2:hb * 2 + 2, :, gt * 128:(gt + 1) * 128].rearrange("h d t -> d h t"),
                                                                   s[:, 0:256].rearrange("d (h t) -> d h t", h=2))], reads=[s], writes=[qbrT_d])
                        transpose_out(ob, 2, dst, None, ob)
                for hb in range(8):
                    wt = nxt
                    if hb + 1 < 8:
                        nxt = load_w(kb, wbuf.next(), w_ukv[:, (hb + 1) * 512:(hb + 2) * 512], 4)
                    for hh in range(2):
                        h = hb * 2 + hh
                        s = fms.next()
                        for (n0, n) in nchunks:
                            p = psm.next()
                            for kc in range(4):
                                kb.op("pe", lambda e, kc=kc, p=p, n0=n0, n=n, hh=hh, wt=wt: e.matmul(p[:, 0:n], wt[:, kc, hh * 256:hh * 256 + 128], ckvT[:, kc, n0:n0 + n], start=(kc == 0), stop=(kc == 3)),
                                      reads=[ckvT, wt], writes=[p], accumulate=(kc > 0))
                            eng = evq.next()
                            if eng == "act":
                                kb.op("act", lambda e, p=p, s=s, n0=n0, n=n: e.activation(out=s[:, n0:n0 + n], in_=p[:, 0:n], func=AF.Copy), reads=[p], writes=[s])
                            else:
                                kb.op("dve", lambda e, p=p, s=s, n0=n0, n=n: e.tensor_copy(out=s[:, n0:n0 + n], in_=p[:, 0:n]), reads=[p], writes=[s])
                        kb.dma("sp", kt_span(4 + h, tok0, s), reads=[s], writes=[kt_d])
                    for ti in range(GT):
                        gt = g0 + ti
                        p = psm.next()
                        for kc in range(4):
                            kb.op("pe", lambda e, kc=kc, p=p, ti=ti, wt=wt: e.matmul(p[:, 0:256], ckvT[:, kc, ti * 128:(ti + 1) * 128],
                                  wt[:, kc, :].rearrange("p (h d) -> p h d", h=2)[:, :, 128:256], start=(kc == 0), stop=(kc == 3)),
                                  reads=[ckvT, wt], writes=[p], accumulate=(kc > 0))
                        ob = tmb.next()
                        eng = evq.next()
                        if eng == "act":
                            kb.op("act", lambda e, ob=ob, p=p: e.activation(out=ob[:, 0:256], in_=p[:, 0:256], func=AF.Copy), reads=[p], writes=[ob])
                        else:
                            kb.op("dve", lambda e, ob=ob, p=p: e.tensor_copy(out=ob[:, 0:256], in_=p[:, 0:256]), reads=[p], writes=[ob])
                        kb.dma("sp", [(v_dst(gt, 2 + hb), ob[:, 0:256])], reads=[ob], writes=[v_d])
                kb.barrier()

GRID_W = 64
def rope_tables(tok_idx, rot_dim):
    q = rot_dim // 4
    inv = (np.float32(10000.0) ** (-np.arange(q, dtype=np.float32) / np.float32(q))).astype(np.float32)
    row = (tok_idx // GRID_W).astype(np.float32); col = (tok_idx % GRID_W).astype(np.float32)
    ang = np.concatenate([row[:, None] * inv[None], col[:, None] * inv[None]], axis=1).astype(np.float32)
    return np.concatenate([np.cos(ang), np.sin(ang)], axis=1).astype(np.float32)

def core_rope(t, rot_dim):
    lat = rope_tables(np.arange(t * 2048, (t + 1) * 2048), rot_dim)
    ctx = np.concatenate([np.ones((256, rot_dim // 2), np.float32), np.zeros((256, rot_dim // 2), np.float32)], axis=1)
    return np.ascontiguousarray(np.concatenate([lat, ctx], axis=0))

def pc(v):
    return np.ascontiguousarray(v.reshape(32, 128).T)

def a0_maps(x, ctx, mod, inp):
    maps = []
    for core in range(8):
        b, t = core // 4, core % 4
        xin = np.ascontiguousarray(np.concatenate([x[b, t * 2048:(t + 1) * 2048], ctx[b]], axis=0))
        m_lat = mod[b, 0]; m_ctx = mod[2, 0]
        modT = np.stack([pc(inp["norm_g"][0]), pc(m_lat[4096:8192]), pc(m_lat[0:4096]), pc(m_ctx[4096:8192]), pc(m_ctx[0:4096])], axis=1)
        gains = np.concatenate([inp["att_qn_g"][0], inp["att_kn_g"][0], inp["mla_cq_g"][0], inp["mla_ckv_g"][0]])[None]
        maps.append({"xin": xin, "modT": np.ascontiguousarray(modT), "w_in": inp["att_w_in"][0], "gains": np.ascontiguousarray(gains),
                     "w_uq": inp["mla_w_uq"][0], "w_ukv": inp["mla_w_ukv"][0], "ropeA": core_rope(t, 128), "ropeB": core_rope(t, 64)})
    return maps
NKEY = 8448; NKB = 66
def outproj_residual(P, kb, yT_src, w_out, x_in, gm_loader, x_out, x_out_d, psm, sel_fn, GT=9, ntiles=NT, ngm=2):
    NG = ntiles // GT; GTOK = GT * 128
    with ScopeStack(kb) as sc:
        yTs = P.sb("yTs", [128, 32, GTOK], BF16, sc)
        wbuf = Rot([P.sb(f"wo{i}", [128, 16384], BF16, sc) for i in range(2)])
        gm = P.sb("gm", [128, ngm, 4096], F32, sc)
        xt = Rot([P.sb(f"xr{i}", [128, 512], F32, sc) for i in range(3)])
        tm = Rot([P.sb(f"tm{i}", [128, 512], F32, sc) for i in range(2)])
        xo = Rot([P.sb(f"xo{i}", [128, 512], F32, sc) for i in range(3)])
        gtmp = P.sb("gtmp", [128, 4096], F32, sc)
        gm_loader(gm, gtmp)
        for g in range(NG):
            tok0 = g * GTOK
            kb.dma("sp", [(yTs[:, 8 * i:8 * i + 8, :], yT_src(8 * i, 8 * i + 8, tok0, GTOK)) for i in range(4)], writes=[yTs])
            nxt = load_w(kb, wbuf.next(), w_out[:, 0:512], 32)
            for cb in range(8):
                wt = nxt
                if cb + 1 < 8:
                    nxt = load_w(kb, wbuf.next(), w_out[:, (cb + 1) * 512:(cb + 2) * 512], 32)
                for ti in range(GT):
                    gt = g * GT + ti
                    x = xt.next()
                    kb.dma("sp", [(x[:], x_in[gt * 128:(gt + 1) * 128, cb * 512:(cb + 1) * 512])], writes=[x])
                    p = psm.next()
                    for kc in range(32):
                        kb.op("pe", lambda e, kc=kc, p=p, ti=ti, wt=wt: e.matmul(p[:, :], yTs[:, kc, ti * 128:(ti + 1) * 128], wt[:, kc, :], start=(kc == 0), stop=(kc == 31)),
                              reads=[yTs, wt], writes=[p], accumulate=(kc > 0))
                    t = tm.next(); o = xo.next()
                    sel = sel_fn(gt)
                    kb.op("dve", lambda e, t=t, p=p, sel=sel, cb=cb: e.tensor_tensor(out=t[:], in0=p[:, :], in1=gm[:, sel, cb * 512:(cb + 1) * 512], op=ALU.mult), reads=[p, gm], writes=[t])
                    kb.op("pool", lambda e, t=t, o=o, x=x: e.tensor_tensor(out=o[:], in0=t[:], in1=x[:], op=ALU.add), reads=[t, x], writes=[o])
                    kb.dma("sp", [(x_out[gt * 128:(gt + 1) * 128, cb * 512:(cb + 1) * 512], o[:])], reads=[o], writes=[x_out_d])
        kb.barrier()

def emit_b0(P, kb, io, gm_loader):
    qaT, _ = io["qaT"]; qbnT, _ = io["qbnT"]; qbrT, _ = io["qbrT"]; gT, _ = io["gT"]
    ktall, _ = io["ktall"]; ktctx, _ = io["ktctx"]; vall, _ = io["vall"]; vctx, _ = io["vctx"]
    w_out = io["att_w_out"]; x_in = io["xin"]; x1, x1_d = io["x1s"]
    yT, yT_d = io["yT"]
    def k_pieces(K, ch):
        return [(K[:, r * 2048:(r + 1) * 2048], ktall[ch, r * 128:(r + 1) * 128, :]) for r in range(4)] + [(K[:, 8192:8448], ktctx[ch, :, :])]
    def v_pieces(V, ch, c0):
        return [(V[:, 0:64, :], vall[ch, :, c0:c0 + 128].rearrange("(k p) d -> p k d", p=128)), (V[:, 64:66, :], vctx[ch, :, c0:c0 + 128].rearrange("(k p) d -> p k d", p=128))]
    with ScopeStack(kb) as scb:
        psm = Rot([P.ps(f"psm{i}", [128, 512], F32, scb) for i in range(4)])
        pso = Rot([P.ps(f"pso{i}", [128, 512], F32, scb) for i in range(2)])
        psr = P.ps("psr", [128, 512], F32, scb)
        qtiles = [(0, 512, 0, NKB), (512, 512, 0, NKB), (1024, 512, 0, NKB), (1536, 512, 0, NKB), (2048, 256, 64, NKB)]
        with ScopeStack(kb) as sc:
            ones = P.sb("ones", [128, 128], F32, sc)
            kb.op("pool", lambda e: e.memset(ones[:], 1.0), writes=[ones])
            Kt = Rot([P.sb(f"Kt{i}", [128, NKEY], BF16, sc) for i in range(2)])
            Vt = Rot([P.sb(f"Vt{i}", [128, NKB, 128], BF16, sc) for i in range(2)])
            Krt = P.sb("Krt", [128, NKEY], BF16, sc)
            Qt = Rot([P.sb(f"Qt{i}", [128, NTOK], BF16, sc) for i in range(2)])
            Qrt = Rot([P.sb(f"Qrt{i}", [128, NTOK], BF16, sc) for i in range(2)])
            Gt = Rot([P.sb(f"Gt{i}", [128, NTOK], BF16, sc) for i in range(2)])
            pT = Rot([P.sb(f"pT{i}", [128, 512], BF16, sc) for i in range(6)])
            accA = Rot([P.sb(f"accA{i}", [128, 512], F32, sc) for i in range(2)])
            accB = Rot([P.sb(f"accB{i}", [128, 512], F32, sc) for i in range(2)])
            accs = P.sb("accs", [128, 512], F32, sc); rinv = P.sb("rinv", [128, 512], F32, sc)
            yf = Rot([P.sb(f"yf{i}", [128, 512], F32, sc) for i in range(2)])
            yb = Rot([P.sb(f"yb{i}", [128, 512], BF16, sc) for i in range(2)])
            kb.dma("sp", [(Krt[hf * 64:hf * 64 + 64, r * 2048:(r + 1) * 2048], ktall[20, r * 128:r * 128 + 64, :]) for hf in range(2) for r in range(4)] + [(Krt[hf * 64:hf * 64 + 64, 8192:8448], ktctx[20, 0:64, :]) for hf in range(2)], writes=[Krt])

            def attend(head, K, V, Q, Qr, half, scale):
                G = Gt.next()
                kb.dma("sp", [(G[:], gT[head, :, :])], writes=[G])
                def do_tile(q0, qn, kb0, kb1):
                    po = pso.next()
                    aA = accA.next(); aB = accB.next()
                    nblk = kb1 - kb0
                    def qk(i, kbi):
                        ps_ = psm.next()
                        kb.op("pe", lambda e: e.matmul(ps_[:, 0:qn], K[:, kbi * 128:(kbi + 1) * 128], Q[:, q0:q0 + qn], start=True, stop=(Qr is None)),
                              reads=[K, Q], writes=[ps_])
                        if Qr is not None:
                            r0 = 64 * half
                            kb.op("pe", lambda e: e.matmul(ps_[:, 0:qn], Krt[r0:r0 + 64, kbi * 128:(kbi + 1) * 128], Qr[r0:r0 + 64, q0:q0 + qn], start=False, stop=True),
                                  reads=[Krt, Qr], writes=[ps_], accumulate=True)
                        return ps_
                    def rest(i, kbi, ps_):
                        pt = pT.next()
                        kb.op("act", lambda e: e.activation(out=pt[:, 0:qn], in_=ps_[:, 0:qn], func=AF.Exp, scale=scale), reads=[ps_], writes=[pt])
                        acc, eng = (aA, "pool") if i % 2 == 0 else (aB, "dve")
                        if i < 2:
                            kb.op(eng, lambda e: e.tensor_copy(out=acc[:, 0:qn], in_=pt[:, 0:qn]), reads=[pt], writes=[acc])
                        else:
                            kb.op(eng, lambda e: e.tensor_tensor(out=acc[:, 0:qn], in0=acc[:, 0:qn], in1=pt[:, 0:qn], op=ALU.add), reads=[pt, acc], writes=[acc])
                        kb.op("pe", lambda e: e.matmul(po[:, 0:qn], V[:, kbi, :], pt[:, 0:qn], start=(i == 0), stop=(i == nblk - 1)),
                              reads=[V, pt], writes=[po], accumulate=(i > 0))
                    LOOK = 2
                    pend = []
                    for i, kbi in enumerate(range(kb0, kb1)):
                        pend.append((i, kbi, qk(i, kbi)))
                        if len(pend) > LOOK:
                            rest(*pend.pop(0))
                    while pend:
                        rest(*pend.pop(0))
                    kb.op("dve", lambda e, aA=aA, aB=aB: e.tensor_tensor(out=accs[:, 0:qn], in0=aA[:, 0:qn], in1=aB[:, 0:qn], op=ALU.add), reads=[aA, aB], writes=[accs])
                    kb.op("pe", lambda e: e.matmul(psr[:, 0:qn], ones[:, :], accs[:, 0:qn], start=True, stop=True), reads=[ones, accs], writes=[psr])
                    kb.op("dve", lambda e: e.reciprocal(out=rinv[:, 0:qn], in_=psr[:, 0:qn]), reads=[psr], writes=[rinv])
                    y1 = yf.next(); y2 = yb.next()
                    kb.op("dve", lambda e, po=po, y1=y1: e.tensor_tensor(out=y1[:, 0:qn], in0=po[:, 0:qn], in1=rinv[:, 0:qn], op=ALU.mult), reads=[po, rinv], writes=[y1])
                    kb.op("pool", lambda e, y1=y1, y2=y2, G=G: e.tensor_tensor(out=y2[:, 0:qn], in0=y1[:, 0:qn], in1=G[:, q0:q0 + qn], op=ALU.mult), reads=[y1, G], writes=[y2])
                    kb.dma("sp", [(yT[head, :, q0:q0 + qn], y2[:, 0:qn])], reads=[y2], writes=[yT_d])
                for qt in qtiles:
                    do_tile(*qt)

            for kvh in range(4):
                K = Kt.next(); V = Vt.next()
                kb.dma("sp", k_pieces(K, kvh), writes=[K])
                kb.dma("sp", v_pieces(V, kvh // 2, (kvh % 2) * 128), writes=[V])
                for qh in range(4):
                    head = kvh * 4 + qh
                    Q = Qt.next()
                    kb.dma("sp", [(Q[:], qaT[head, :, :])], writes=[Q])
                    attend(head, K, V, Q, None, 0, 128.0 ** -0.5)
            for h in range(16):
                K = Kt.next(); V = Vt.next()
                kb.dma("sp", k_pieces(K, 4 + h), writes=[K])
                kb.dma("sp", v_pieces(V, 2 + h // 2, (h % 2) * 128), writes=[V])
                Q = Qt.next()
                kb.dma("sp", [(Q[:], qbnT[h, :, :])], writes=[Q])
                if h % 2 == 0:
                    Qr = Qrt.next()
                    kb.dma("sp", [(Qr[:], qbrT[h // 2, :, :])], writes=[Qr])
                attend(16 + h, K, V, Q, Qr, h % 2, 192.0 ** -0.5)
            kb.barrier()
        def yT_src(b0, b1, tok0, n):
            return yT[b0:b1, :, tok0:tok0 + n].rearrange("b p t -> p b t")
        psm2 = Rot([psm.items[0], psm.items[1], psm.items[2], psm.items[3], pso.items[0], pso.items[1], psr])
        kb.wait_all("sp", [yT_d])
        outproj_residual(P, kb, yT_src, w_out, x_in, gm_loader, x1, x1_d, psm2, lambda gt: 0 if gt < 16 else 1)

def emit_a1(P, kb, io, G, Sh, identb, identf, GT=6):
    NG = NT // GT; GTOK = GT * 128
    nchunks = [(i, min(512, GTOK - i)) for i in range(0, GTOK, 512)]
    xin = io["x1s"][0]; w_in = io["rec_w_in"]; dftc = io["dftc"]; modT = None
    qT, qT_d = io["qT"]; kT, kT_d = io["kT"]; v, v_d = io["v"]; gdT, gdT_d = io["gdT"]
    ucs, ucs_d = io["ucs_send"]; sgl, sgl_d = io["sgl"]; sgfT, sgfT_d = io["sgfT"]
    with ScopeStack(kb) as sc0:
        dc = P.sb("dc", [128, 2, 512], BF16, sc0)
        kb.dma("sp", [(dc[:], dftc.rearrange("(c p) n -> p c n", p=128))], writes=[dc])
        hT = P.sb("hT", [128, 32, GTOK], BF16, sc0)
        uT = P.sb("uT", [128, 8, GTOK], BF16, sc0)
        psm = Rot([P.ps(f"psm{i}", [128, 512], F32, sc0) for i in range(6)])
        evq = Rot(["act", "dve"])
        for g in range(NG):
            g0 = g * GT; tok0 = g0 * 128
            with ScopeStack(kb) as sc1:
                rms_prep(P, kb, GT, g0, xin, modT, lambda gt: gt < 16, hT, psm, identf, sc1, G, Sh, evq)
                kb.barrier()
            with ScopeStack(kb) as sc2:
                wbuf = Rot([P.sb(f"wbuf{i}", [128, 16384], BF16, sc2) for i in range(2)])
                tmb = Rot([P.sb(f"tmb{i}", [128, 512], BF16, sc2) for i in range(3)])
                fms = Rot([P.sb(f"fms{i}", [128, GTOK], BF16, sc2) for i in range(3)])
                gds = P.sb("gds", [32, GTOK], F32, sc2)

                def evac(o, i, reads, writes, func=AF.Copy):
                    eng = evq.next() if func == AF.Copy else "act"
                    if eng == "act":
                        kb.op("act", lambda e: e.activation(out=o, in_=i, func=func), reads=reads, writes=writes)
                    else:
                        kb.op("dve", lambda e: e.tensor_copy(out=o, in_=i), reads=reads, writes=writes)

                def fm_cols(wt, c0, M, dst_fn, func=AF.Copy, stage=None):
                    s = stage if stage is not None else fms.next()
                    for (n0, n) in nchunks:
                        p = psm.next()
                        for kc in range(32):
                            kb.op("pe", lambda e, kc=kc, p=p, n0=n0, n=n: e.matmul(p[0:M, 0:n], wt[:, kc, c0:c0 + M], hT[:, kc, n0:n0 + n], start=(kc == 0), stop=(kc == 31)),
                                  reads=[hT, wt], writes=[p], accumulate=(kc > 0))
                        evac(s[0:M, n0:n0 + n], p[0:M, 0:n], [p], [s], func)
                    dst_fn(s)

                def tm_cols(wt, ncols, dst, dst_d, c_out, func=AF.Copy):
                    for ti in range(GT):
                        gt = g0 + ti
                        p = psm.next()
                        for kc in range(32):
                            kb.op("pe", lambda e, kc=kc, p=p, ti=ti: e.matmul(p[:, 0:ncols], hT[:, kc, ti * 128:(ti + 1) * 128], wt[:, kc, 0:ncols], start=(kc == 0), stop=(kc == 31)),
                                  reads=[hT, wt], writes=[p], accumulate=(kc > 0))
                        ob = tmb.next()
                        evac(ob[:, 0:ncols], p[:, 0:ncols], [p], [ob], func)
                        kb.dma("sp", [(dst[gt * 128:(gt + 1) * 128, c_out:c_out + ncols], ob[:, 0:ncols])], reads=[ob], writes=[dst_d])

                blocks = [("q", 0, 512), ("q", 512, 512), ("q", 1024, 512), ("k", 1536, 512), ("k", 2048, 512), ("k", 2560, 512)] + \
                         [("v", 3072 + 512 * i, 512) for i in range(6)] + [("gd", 6144, 32), ("u", 6176, 512), ("u", 6688, 512)] + \
                         [("sgl", 7200 + 512 * i, 512) for i in range(6)] + [("sgf", 10272, 512), ("sgf", 10784, 512)]
                nxt = load_w(kb, wbuf.next(), w_in[:, blocks[0][1]:blocks[0][1] + blocks[0][2]], 32)
                cnt = {"q": 0, "k": 0, "v": 0, "u": 0, "sgl": 0, "sgf": 0}
                for bi, (kind, c0, ncols) in enumerate(blocks):
                    wt = nxt
                    if bi + 1 < len(blocks):
                        nxt = load_w(kb, wbuf.next(), w_in[:, blocks[bi + 1][1]:blocks[bi + 1][1] + blocks[bi + 1][2]], 32)
                    if kind in ("q", "k", "sgf"):
                        dstT, dstT_d = {"q": (qT, qT_d), "k": (kT, kT_d), "sgf": (sgfT, sgfT_d)}[kind]
                        for sbk in range(4):
                            blk = cnt[kind]; cnt[kind] += 1
                            fm_cols(wt, sbk * 128, 128, lambda s, blk=blk, dstT=dstT, dstT_d=dstT_d: kb.dma("sp", [(dstT[blk, :, tok0:tok0 + GTOK], s[:, :])], reads=[s], writes=[dstT_d]),
                                    func=(AF.Silu if kind == "sgf" else AF.Copy))
                    elif kind == "u":
                        for sbk in range(4):
                            blk = cnt["u"]; cnt["u"] += 1
                            uview = T(uT[:, blk, :], "uv"); uview.d = uT.d
                            fm_cols(wt, sbk * 128, 128, lambda s: None, stage=uview)
                    elif kind == "gd":
                        fm_cols(wt, 0, 32, lambda s: kb.dma("sp", [(gdT[:, tok0:tok0 + GTOK], s[0:32, :])], reads=[s], writes=[gdT_d]), stage=gds)
                    elif kind == "v":
                        tm_cols(wt, 512, v, v_d, cnt["v"] * 512); cnt["v"] += 1
                    elif kind == "sgl":
                        tm_cols(wt, 512, sgl, sgl_d, cnt["sgl"] * 512, func=AF.Silu); cnt["sgl"] += 1
                for ti in range(GT):
                    gt = g0 + ti
                    if gt >= 16: continue
                    for grp in range(4):
                        p = psm.next()
                        for cc in range(2):
                            kb.op("pe", lambda e, cc=cc, p=p, ti=ti, grp=grp: e.matmul(p[:, :], uT[:, 2 * grp + cc, ti * 128:(ti + 1) * 128], dc[:, cc, :], start=(cc == 0), stop=(cc == 1)),
                                  reads=[uT, dc], writes=[p], accumulate=(cc > 0))
                        ob = tmb.next()
                        evac(ob[:, :], p[:, :], [p], [ob])
                        kb.dma("sp", [(ucs[grp * 2 + hf, gt * 128:(gt + 1) * 128, :], ob[:, hf * 256:(hf + 1) * 256]) for hf in range(2)], reads=[ob], writes=[ucs_d])
                kb.barrier()

def dft_c_table():
    c = np.arange(256)
    ang = 2 * np.pi * ((c[:, None] * c[None, :]) % 256) / 256.0
    return np.concatenate([np.cos(ang), np.sin(ang)], axis=1).astype(NPBF)

def emit_gla(P, kb, io, identb, identf):
    qT, _ = io["qT"]; kT, _ = io["kT"]; v, _ = io["v"]; gdT, _ = io["gdT"]
    wg_in = io["gla_wg"]; bg_in = io["gla_bgT"]; msel_in = io["msel"]; valid_in = io["valid"]
    of_s, of_d = io["of_s"]; ob_s, ob_d = io["ob_s"]
    sloc_send, sloc_send_d = io["sloc_send"]; sloc_all, sloc_all_d = io["sloc_all"]
    la_send, la_send_d = io["la_send"]; la_all, la_all_d = io["la_all"]
    sctx, sctx_d = io["sctx"]
    SCALE = 256.0 ** -0.5
    with ScopeStack(kb) as sc:
        maskF = P.sb("maskF", [128, 128], F32, sc); maskB = P.sb("maskB", [128, 128], F32, sc); zeros = P.sb("zeros", [128, 128], F32, sc)
        kb.op("pool", lambda e: e.memset(maskF[:], 1.0), writes=[maskF])
        kb.op("pool", lambda e: e.affine_select(out=maskF[:], in_=maskF[:], pattern=[[1, 128]], compare_op=ALU.is_ge, fill=0.0, base=0, channel_multiplier=-1), reads=[maskF], writes=[maskF])
        kb.op("pool", lambda e: e.memset(maskB[:], 1.0), writes=[maskB])
        kb.op("pool", lambda e: e.affine_select(out=maskB[:], in_=maskB[:], pattern=[[-1, 128]], compare_op=ALU.is_ge, fill=0.0, base=0, channel_multiplier=1), reads=[maskB], writes=[maskB])
        kb.op("pool", lambda e: e.memset(zeros[:], 0.0), writes=[zeros])
        wg = [P.sb(f"wg{d}", [16, 1536], F32, sc) for d in range(2)]
        nbg = P.sb("nbg", [128, 2, 12], F32, sc)
        for d in range(2):
            kb.dma("sp", [(wg[d][:], wg_in[d, :, :])], writes=[wg[d]])
        kb.dma("sp", [(nbg[:], bg_in)], writes=[nbg])
        kb.op("dve", lambda e: e.tensor_scalar(out=nbg[:], in0=nbg[:], scalar1=-1.0, scalar2=None, op0=ALU.mult), reads=[nbg], writes=[nbg])
        gd = [P.sb(f"gd{d}", [16, NTOK], F32, sc) for d in range(2)]
        for d in range(2):
            kb.dma("sp", [(gd[d][:], gdT[16 * d:16 * d + 16, :])], writes=[gd[d]])
        latot = P.sb("latot", [128, 24], F32, sc)
        kb.op("pool", lambda e: e.memset(latot[:], 0.0), writes=[latot])
        msel = P.sb("msel", [4, 2, 5], F32, sc); valid = P.sb("valid", [128, 2, 5], F32, sc)
        kb.dma("sp", [(msel[:], msel_in)], writes=[msel]); kb.dma("sp", [(valid[:], valid_in)], writes=[valid])
        coefs = P.sb("coefs", [128, 24, 5], F32, sc)
        LAt = P.sb("LAt", [4, 24, 128], F32, sc)
        pz = P.ps("pz", [128, 512], F32, sc); patt = P.ps("patt", [128, 512], F32, sc); pkd = P.ps("pkd", [128, 512], BF16, sc)
        po = Rot([P.ps(f"po{i}", [128, 512], F32, sc) for i in range(2)])
        pdS = [P.ps(f"pdS{i}", [128, 512], F32, sc) for i in range(2)]
        qh = [P.sb(f"qh{i}", [128, 2, NTOK], BF16, sc) for i in range(2)]
        kh = [P.sb(f"kh{i}", [128, 2, NTOK], BF16, sc) for i in range(2)]
        vh = [P.sb(f"vh{i}", [128, NT, 512], BF16, sc) for i in range(2)]
        srcb = Rot([P.sb(f"srcb{i}", [128, 512], F32, sc) for i in range(4)])
        class Chain: pass
        chains = []
        for ci in range(4):
            pr = Chain(); pr.ci = ci
            pr.S = P.sb(f"S{ci}", [128, 2, 512], F32, sc); pr.Sb = P.sb(f"Sb{ci}", [128, 2, 512], BF16, sc)
            for nm, shp, dt in (("e1", [128, 256], F32), ("l1", [128, 256], F32), ("cumf", [128, 256], F32), ("cumn", [128, 256], F32), ("eq", [128, 256], F32), ("ek", [128, 256], F32),
                                ("ed", [128, 256], F32), ("sm", [128, 4], F32), ("qe", [128, 256], BF16), ("ke", [128, 256], BF16), ("kdT", [128, 256], BF16),
                                ("kd", [128, 256], BF16), ("attT", [128, 128], BF16)):
                setattr(pr, nm, P.sb(f"{nm}{ci}", shp, dt, sc))
            pr.osb = Rot([P.sb(f"osb{ci}_{i}", [128, 512], F32, sc) for i in range(2)])
            chains.append(pr)

        def tile_step(pr, gt, full, out_dst):
            h, d, hs = pr.h, pr.d, pr.hs
            sl = slice(gt * 128, (gt + 1) * 128)
            qs, ks, vs, gs = qh[hs], kh[hs], vh[hs], gd[d]
            li = 127 if d == 0 else 0
            mask = maskF if d == 0 else maskB
            for c in range(2):
                kb.op("pe", lambda e, c=c: e.matmul(pz[:, c * 128:(c + 1) * 128], wg[d][:, h * 256 + c * 128:h * 256 + (c + 1) * 128], gs[:, sl], start=True, stop=True), reads=[wg[d], gs], writes=[pz], accumulate=(c > 0))
            for c in range(2):
                kb.op("act", lambda e, c=c: e.activation(out=pr.e1[:, c * 128:(c + 1) * 128], in_=pz[:, c * 128:(c + 1) * 128], func=AF.Exp, scale=-1.0, bias=nbg[:, d, h * 2 + c:h * 2 + c + 1]), reads=[pz, nbg], writes=[pr.e1])
            kb.op("act", lambda e: e.activation(out=pr.l1[:], in_=pr.e1[:], func=AF.Ln, scale=1.0, bias=1.0), reads=[pr.e1], writes=[pr.l1])
            cdst = pr.cumn if d == 0 else pr.cumf
            for c in range(2):
                kb.op("dve", lambda e, c=c: e.tensor_tensor_scan(out=cdst[:, c * 128:(c + 1) * 128], data0=zeros[:], data1=pr.l1[:, c * 128:(c + 1) * 128], initial=0.0, op0=ALU.add, op1=ALU.add),
                      reads=[zeros, pr.l1], writes=[cdst])
            if d == 1:
                for c in range(2):
                    kb.op("dve", lambda e, c=c: e.scalar_tensor_tensor(out=pr.cumn[:, c * 128:(c + 1) * 128], in0=pr.l1[:, c * 128:(c + 1) * 128], scalar=pr.cumf[:, c * 128 + 127:c * 128 + 128],
                                                                     in1=pr.cumf[:, c * 128:(c + 1) * 128], op0=ALU.add, op1=ALU.subtract), reads=[pr.l1, pr.cumf], writes=[pr.cumn])
            lastn = pr.cumn[:].rearrange("p (c t) -> p c t", c=2)[:, :, li]
            if not full:
                q0 = (h * 2 + d) * 2
                kb.op("dve", lambda e: e.tensor_tensor(out=latot[:, q0:q0 + 2], in0=latot[:, q0:q0 + 2], in1=lastn, op=ALU.add), reads=[latot, pr.cumn], writes=[latot])
            kb.op("dve", lambda e: e.tensor_scalar(out=pr.sm[:, 0:2], in0=lastn, scalar1=-1.0 / 16, scalar2=None, op0=ALU.mult), reads=[pr.cumn], writes=[pr.sm])
            kb.op("act", lambda e: e.activation(out=pr.sm[:, 2:4], in_=pr.sm[:, 0:2], func=AF.Exp), reads=[pr.sm], writes=[pr.sm])
            for c in range(2):
                kb.op("act", lambda e, c=c: e.activation(out=pr.ed[:, c * 128:(c + 1) * 128], in_=pr.cumn[:, c * 128:(c + 1) * 128], func=AF.Exp, scale=1.0 / 16, bias=pr.sm[:, c:c + 1]), reads=[pr.cumn, pr.sm], writes=[pr.ed])
            v3 = lambda t: t[:].rearrange("p (c t) -> p c t", c=2)
            kb.op("pool", lambda e: e.tensor_tensor(out=v3(pr.kdT), in0=ks[:, :, sl], in1=v3(pr.ed), op=ALU.mult), reads=[ks, pr.ed], writes=[pr.kdT])
            for c in range(2):
                kb.op("pe", lambda e, c=c: e.transpose(pkd[:, c * 128:(c + 1) * 128], pr.kdT[:, c * 128:(c + 1) * 128], identb[:]), reads=[pr.kdT, identb], writes=[pkd], accumulate=(c > 0))
            kb.op("act", lambda e: e.activation(out=pr.kd[:], in_=pkd[:, 0:256], func=AF.Copy), reads=[pkd], writes=[pr.kd])
            if full:
                kb.op("act", lambda e: e.activation(out=pr.eq[:], in_=pr.cumn[:], func=AF.Exp, scale=-1.0 / 16), reads=[pr.cumn], writes=[pr.eq])
                kb.op("act", lambda e: e.activation(out=pr.ek[:], in_=pr.cumn[:], func=AF.Exp, scale=1.0 / 16), reads=[pr.cumn], writes=[pr.ek])
                kb.op("dve", lambda e: e.scalar_tensor_tensor(out=v3(pr.qe), in0=qs[:, :, sl], scalar=SCALE, in1=v3(pr.eq), op0=ALU.mult, op1=ALU.mult), reads=[qs, pr.eq], writes=[pr.qe])
                kb.op("pool", lambda e: e.tensor_tensor(out=v3(pr.ke), in0=ks[:, :, sl], in1=v3(pr.ek), op=ALU.mult), reads=[ks, pr.ek], writes=[pr.ke])
                for c in range(2):
                    kb.op("pe", lambda e, c=c: e.matmul(patt[:, 0:128], pr.ke[:, c * 128:(c + 1) * 128], pr.qe[:, c * 128:(c + 1) * 128], start=(c == 0), stop=(c == 1)), reads=[pr.ke, pr.qe], writes=[patt], accumulate=(c > 0))
                kb.op("dve", lambda e: e.tensor_tensor(out=pr.attT[:], in0=patt[:, 0:128], in1=mask[:], op=ALU.mult), reads=[patt, mask], writes=[pr.attT])
                p = po.next()
                for c in range(2):
                    kb.op("pe", lambda e, c=c: e.matmul(p[:, :], pr.qe[:, c * 128:(c + 1) * 128], pr.Sb[:, c, :], start=(c == 0), stop=False), reads=[pr.qe, pr.Sb], writes=[p], accumulate=(c > 0))
                kb.op("pe", lambda e: e.matmul(p[:, :], pr.attT[:], vs[:, gt, :], start=False, stop=True), reads=[pr.attT, vs], writes=[p], accumulate=True)
                ob = pr.osb.next()
                kb.op("act", lambda e: e.activation(out=ob[:], in_=p[:, :], func=AF.Copy), reads=[p], writes=[ob])
                dst, dst_d = out_dst
                kb.dma("sp", [(dst[gt * 128:(gt + 1) * 128, h * 512:(h + 1) * 512], ob[:])], reads=[ob], writes=[dst_d])
            for c in range(2):
                kb.op("pe", lambda e, c=c: e.matmul(pdS[c][:, :], pr.kd[:, c * 128:(c + 1) * 128], vs[:, gt, :], start=True, stop=True), reads=[pr.kd, vs], writes=[pdS[c]])
                kb.op("dve", lambda e, c=c: e.scalar_tensor_tensor(out=pr.S[:, c, :], in0=pr.S[:, c, :], scalar=pr.sm[:, 2 + c:3 + c], in1=pdS[c][:, :], op0=ALU.mult, op1=ALU.add), reads=[pr.S, pr.sm, pdS[c]], writes=[pr.S])
            if full:
                kb.op("pool", lambda e: e.tensor_copy(out=pr.Sb[:], in_=pr.S[:]), reads=[pr.S], writes=[pr.Sb])

        def load_heads(hp):
            for hs in range(2):
                h = hp * 2 + hs
                kb.dma("sp", [(qh[hs][:], qT[2 * h:2 * h + 2, :, :].rearrange("c p t -> p c t"))], writes=[qh[hs]])
                kb.dma("sp", [(kh[hs][:], kT[2 * h:2 * h + 2, :, :].rearrange("c p t -> p c t"))], writes=[kh[hs]])
                kb.dma("sp", [(vh[hs][:, 6 * i:6 * i + 6, :], v[6 * i * 128:(6 * i + 6) * 128, h * 512:(h + 1) * 512].rearrange("(t p) d -> p t d", p=128)) for i in range(3)], writes=[vh[hs]])
            for ci, pr in enumerate(chains):
                pr.hs = ci // 2; pr.h = hp * 2 + pr.hs; pr.d = ci % 2; pr.hd = pr.h * 2 + pr.d

        def order(d, tiles):
            return tiles if d == 0 else tiles[::-1]

        for hp in range(3):
            load_heads(hp)
            for pr in chains:
                kb.op("pool", lambda e, pr=pr: e.memset(pr.S[:], 0.0), writes=[pr.S])
            for step in range(2):
                for pr in chains:
                    tile_step(pr, order(pr.d, [16, 17])[step], False, None)
            for pr in chains:
                kb.dma("sp", [(sctx[pr.hd], pr.S[:])], reads=[pr.S], writes=[sctx_d])
                kb.op("pool", lambda e, pr=pr: e.memset(pr.S[:], 0.0), writes=[pr.S])
                q0 = pr.hd * 2
                kb.op("pool", lambda e, q0=q0: e.memset(latot[:, q0:q0 + 2], 0.0), writes=[latot])
            for step in range(16):
                for pr in chains:
                    tile_step(pr, order(pr.d, list(range(16)))[step], False, None)
            for pr in chains:
                kb.dma("sp", [(sloc_send[pr.hd].rearrange("(c p) n -> p c n", p=128), pr.S[:])], reads=[pr.S], writes=[sloc_send_d])
        pla = patt
        kb.op("pe", lambda e: e.transpose(pla[0:24, 0:128], latot[:, 0:24], identf[:]), reads=[latot, identf], writes=[pla])
        las = P.sb("las", [24, 128], F32, sc)
        kb.op("dve", lambda e: e.tensor_copy(out=las[:], in_=pla[0:24, 0:128]), reads=[pla], writes=[las])
        kb.dma("sp", [(la_send[:, :], las[:])], reads=[las], writes=[la_send_d])
        kb.coll("AllGather", ALU.bypass, G4, la_send[:, :], la_send_d, la_all[:, :], la_all_d)
        for hd in range(12):
            kb.coll("AllGather", ALU.bypass, G4, sloc_send[hd], sloc_send_d, sloc_all[hd], sloc_all_d)
        kb.barrier()
        kb.dma("sp", [(LAt[:], la_all.rearrange("(r q) p -> r q p", r=4))], writes=[LAt])
        for q in range(24):
            d = (q // 2) % 2
            kb.op("pe", lambda e, q=q, d=d: e.matmul(pz[:, 0:5], LAt[0:4, q, :], msel[0:4, d, :], start=True, stop=True), reads=[LAt, msel], writes=[pz])
            kb.op("act", lambda e, q=q: e.activation(out=coefs[:, q, :], in_=pz[:, 0:5], func=AF.Exp, scale=-1.0 / 16), reads=[pz], writes=[coefs])
            kb.op("dve", lambda e, q=q, d=d: e.tensor_tensor(out=coefs[:, q, :], in0=coefs[:, q, :], in1=valid[:, d, :], op=ALU.mult), reads=[coefs, valid], writes=[coefs])
        for hp in range(3):
            load_heads(hp)
            for pr in chains:
                for c in range(2):
                    q = pr.hd * 2 + c
                    for s_ in range(5):
                        sb_ = srcb.next()
                        src = sloc_all[pr.hd, s_ * 256 + c * 128:s_ * 256 + (c + 1) * 128, :] if s_ < 4 else sctx[pr.hd, :, c, :]
                        kb.dma("sp", [(sb_[:], src)], writes=[sb_])
                        if s_ == 0:
                            kb.op("dve", lambda e, pr=pr, c=c, q=q, sb_=sb_: e.tensor_scalar(out=pr.S[:, c, :], in0=sb_[:], scalar1=coefs[:, q, 0:1], scalar2=None, op0=ALU.mult), reads=[sb_, coefs], writes=[pr.S])
                        else:
                            kb.op("dve", lambda e, pr=pr, c=c, q=q, sb_=sb_, s_=s_: e.scalar_tensor_tensor(out=pr.S[:, c, :], in0=sb_[:], scalar=coefs[:, q, s_:s_ + 1], in1=pr.S[:, c, :], op0=ALU.mult, op1=ALU.add), reads=[sb_, coefs, pr.S], writes=[pr.S])
                kb.op("pool", lambda e, pr=pr: e.tensor_copy(out=pr.Sb[:], in_=pr.S[:]), reads=[pr.S], writes=[pr.Sb])
            for step in range(16):
                for pr in chains:
                    tile_step(pr, order(pr.d, list(range(16)))[step], True, (of_s, of_d) if pr.d == 0 else (ob_s, ob_d))
        kb.barrier()

def gla_core_tables(t):
    msel = np.zeros((4, 2, 5), np.float32); valid = np.zeros((2, 5), np.float32)
    for s in range(4):
        if s < t:
            valid[0, s] = 1.0
            for tt in range(s + 1, t): msel[tt, 0, s] = 1.0
        if s > t:
            valid[1, s] = 1.0
            for tt in range(t + 1, s): msel[tt, 1, s] = 1.0
    valid[0, 4] = 1.0; valid[1, 4] = 1.0
    for tt in range(0, t): msel[tt, 0, 4] = 1.0
    for tt in range(t + 1, 4): msel[tt, 1, 4] = 1.0
    return msel, np.ascontiguousarray(np.broadcast_to(valid[None], (128, 2, 5)))
def emit_c1(P, kb, io, identb, identf, gm_loader):
    NL = 2048; NLT = 16
    of, _ = io["of_s"]; ob, _ = io["ob_s"]; on_g = io["on_g"]; sgl, _ = io["sgl"]; sgfT, _ = io["sgfT"]
    UCS, _ = io["ucs_all"]; tabP = io["tabP"]; w_out = io["rec_w_out"]; x_in = io["x1s"][0]; final_g = io["final_g"]
    out, out_d = io["out"]; yT, yT_d = io["yT"]; x2, x2_d = io["x2"]
    with ScopeStack(kb) as scc:
        psm = Rot([P.ps(f"psm{i}", [128, 512], F32, scc) for i in range(6)])
        pst = Rot([P.ps(f"pst{i}", [128, 512], BF16, scc) for i in range(2)])
        evq = Rot(["act", "dve"])
        with ScopeStack(kb) as sc:
            tab = [P.sb(f"tab{i}", [128, 64, 512], BF16, sc) for i in range(2)]
            ub = Rot([P.sb(f"ub{i}", [128, 2, 64, 128], BF16, sc) for i in range(2)])
            sg = Rot([P.sb(f"sg{i}", [128, 512], BF16, sc) for i in range(2)])
            ys = Rot([P.sb(f"ys{i}", [128, 512], BF16, sc) for i in range(2)])
            for pg in range(4):
                for s_ in range(2):
                    src = tabP[s_, :, pg * 512:(pg + 1) * 512].rearrange("(c p) n -> p c n", p=128)
                    kb.dma("sp", [(tab[s_][:, 16 * i:16 * i + 16, :], src[:, 16 * i:16 * i + 16, :]) for i in range(4)], writes=[tab[s_]])
                for fb in range(8):
                    u = ub.next(); grp = fb // 2; c0 = (fb % 2) * 128
                    kb.dma("sp", [(u[:, s_, 32 * i:32 * i + 32, :], UCS[grp * 2 + s_, :, c0:c0 + 128].rearrange("(c p) n -> p c n", p=128)[:, 32 * i:32 * i + 32, :])
                                  for s_ in range(2) for i in range(2)], writes=[u])
                    g_ = sg.next()
                    kb.dma("sp", [(g_[:], sgfT[fb, :, pg * 512:(pg + 1) * 512])], writes=[g_])
                    p = psm.next()
                    for s_ in range(2):
                        for c in range(64):
                            first = (s_ == 0 and c == 0); last = (s_ == 1 and c == 63)
                            kb.op("pe", lambda e, s_=s_, c=c, u=u, p=p, first=first, last=last: e.matmul(p[:, :], u[:, s_, c, :], tab[s_][:, c, :], start=first, stop=last),
                                  reads=[u, tab[s_]], writes=[p], accumulate=(not first))
                    y_ = ys.next()
                    kb.op("dve", lambda e, y_=y_, p=p, g_=g_: e.tensor_tensor(out=y_[:], in0=p[:, :], in1=g_[:], op=ALU.mult), reads=[p, g_], writes=[y_])
                    kb.dma("sp", [(yT[24 + fb, :, pg * 512:(pg + 1) * 512], y_[:])], reads=[y_], writes=[yT_d])
            kb.barrier()
        with ScopeStack(kb) as sc:
            gb = P.sb("ong", [128, 512], F32, sc)
            kb.dma("sp", [(gb[:], on_g.partition_broadcast(128))], writes=[gb])
            oa = Rot([P.sb(f"oa{i}", [128, 3072], F32, sc) for i in range(2)])
            obt = Rot([P.sb(f"obt{i}", [128, 3072], F32, sc) for i in range(2)])
            sgt = Rot([P.sb(f"sgt{i}", [128, 3072], BF16, sc) for i in range(2)])
            sq = P.sb("sq", [128, 3072], F32, sc); st = P.sb("st", [128, 16], F32, sc)
            yb = Rot([P.sb(f"ybf{i}", [128, 3072], BF16, sc) for i in range(2)])
            trs = Rot([P.sb(f"trs{i}", [128, 24, 128], BF16, sc) for i in range(2)])
            for ti in range(NLT):
                a = oa.next(); b_ = obt.next(); g_ = sgt.next()
                rs = slice(ti * 128, (ti + 1) * 128)
                kb.dma("sp", [(a[:, 1024 * i:1024 * (i + 1)], of[rs, 1024 * i:1024 * (i + 1)]) for i in range(3)], writes=[a])
                kb.dma("sp", [(b_[:, 1024 * i:1024 * (i + 1)], ob[rs, 1024 * i:1024 * (i + 1)]) for i in range(3)], writes=[b_])
                kb.dma("sp", [(g_[:], sgl[rs, :])], writes=[g_])
                kb.op("dve", lambda e, a=a, b_=b_: e.tensor_tensor(out=a[:], in0=a[:], in1=b_[:], op=ALU.add), reads=[a, b_], writes=[a])
                kb.op("pool", lambda e, a=a: e.tensor_tensor(out=sq[:], in0=a[:], in1=a[:], op=ALU.mult), reads=[a], writes=[sq])
                kb.op("dve", lambda e: e.tensor_reduce(out=st[:, 0:6], in_=sq[:].rearrange("p (h d) -> p h d", h=6), axis=AX.X, op=ALU.add), reads=[sq], writes=[st])
                kb.op("act", lambda e: e.activation(out=st[:, 8:14], in_=st[:, 0:6], func=AF.Sqrt, scale=1.0 / 512, bias=EPS), reads=[st], writes=[st])
                kb.op("dve", lambda e: e.reciprocal(out=st[:, 8:14], in_=st[:, 8:14]), reads=[st], writes=[st])
                a3 = a[:].rearrange("p (h d) -> p h d", h=6)
                kb.op("dve", lambda e, a3=a3: e.tensor_tensor(out=a3, in0=a3, in1=st[:, 8:14].unsqueeze(2).to_broadcast([128, 6, 512]), op=ALU.mult), reads=[a, st], writes=[a])
                kb.op("pool", lambda e, a3=a3: e.tensor_tensor(out=a3, in0=a3, in1=gb[:].unsqueeze(1).to_broadcast([128, 6, 512]), op=ALU.mult), reads=[a, gb], writes=[a])
                y_ = yb.next()
                kb.op("dve", lambda e, a=a, g_=g_, y_=y_: e.tensor_tensor(out=y_[:], in0=a[:], in1=g_[:], op=ALU.mult), reads=[a, g_], writes=[y_])
                s_ = trs.next()
                for q4 in range(6):
                    p = pst.next()
                    for jj in range(4):
                        blk = q4 * 4 + jj
                        kb.op("pe", lambda e, p=p, jj=jj, blk=blk, y_=y_: e.transpose(p[:, jj * 128:(jj + 1) * 128], y_[:, blk * 128:(blk + 1) * 128], identb[:]), reads=[y_, identb], writes=[p], accumulate=(jj > 0))
                    eng = evq.next()
                    o_ = s_[:, q4 * 4:q4 * 4 + 4, :]
                    i_ = p[:, :].rearrange("p (b t) -> p b t", b=4)
                    if eng == "act":
                        kb.op("act", lambda e, o_=o_, i_=i_: e.activation(out=o_, in_=i_, func=AF.Copy), reads=[p], writes=[s_])
                    else:
                        kb.op("dve", lambda e, o_=o_, i_=i_: e.tensor_copy(out=o_, in_=i_), reads=[p], writes=[s_])
                kb.dma("sp", [(yT[0:24, :, rs].rearrange("b p t -> p b t"), s_[:])], reads=[s_], writes=[yT_d])
            kb.barrier()
        def yT_src(b0, b1, tok0, n):
            return yT[b0:b1, :, tok0:tok0 + n].rearrange("b p t -> p b t")
        kb.wait_all("sp", [yT_d])
        outproj_residual(P, kb, yT_src, w_out, x_in, gm_loader, x2, x2_d, psm, lambda gt: 0, GT=8, ntiles=NLT, ngm=1)
        kb.wait_all("sp", [x2_d])
        with ScopeStack(kb) as sc:
            fg = P.sb("fg", [128, 4096], F32, sc)
            kb.dma("sp", [(fg[:], final_g.partition_broadcast(128))], writes=[fg])
            xt = Rot([P.sb(f"xf{i}", [128, 4096], F32, sc) for i in range(2)])
            yt = Rot([P.sb(f"yf{i}", [128, 4096], F32, sc) for i in range(2)])
            ss = P.sb("ssf", [128, 2], F32, sc)
            for ti in range(NLT):
                rs = slice(ti * 128, (ti + 1) * 128)
                x = xt.next(); y = yt.next()
                kb.dma("sp", [(x[:, 1024 * i:1024 * (i + 1)], x2[rs, 1024 * i:1024 * (i + 1)]) for i in range(4)], reads=[x2_d], writes=[x], track=x.d)
                kb.op("act", lambda e, x=x, y=y: e.activation(out=y[:], in_=x[:], func=AF.Square, accum_out=ss[:, 0:1]), reads=[x], writes=[y, ss])
                kb.op("act", lambda e: e.activation(out=ss[:, 1:2], in_=ss[:, 0:1], func=AF.Sqrt, scale=1.0 / 4096, bias=EPS), reads=[ss], writes=[ss])
                kb.op("dve", lambda e: e.reciprocal(out=ss[:, 1:2], in_=ss[:, 1:2]), reads=[ss], writes=[ss])
                kb.op("dve", lambda e, x=x, y=y: e.scalar_tensor_tensor(out=y[:], in0=x[:], scalar=ss[:, 1:2], in1=fg[:], op0=ALU.mult, op1=ALU.mult), reads=[x, ss, fg], writes=[y])
                kb.dma("sp", [(out[rs, 1024 * i:1024 * (i + 1)], y[:, 1024 * i:1024 * (i + 1)]) for i in range(4)], reads=[y], writes=[out_d])

def dft_p_table(t):
    p = np.arange(8192, dtype=np.int64)[:, None]; pp = np.arange(t * 2048, (t + 1) * 2048, dtype=np.int64)[None, :]
    ang = (2 * np.pi / 8192.0) * ((p * pp) % 8192)
    sc = 1.0 / np.sqrt(8192.0 * 256.0)
    return np.stack([(sc * np.cos(ang)).astype(NPBF), (-sc * np.sin(ang)).astype(NPBF)])

def emit_l1(P, kb, io, G1, Sh1, identb, identf, mod_all, bsel):
    def scr(name, shape, dt=F32):
        io[name] = P.scr(name, shape, dt)
    io["rec_w_in"] = P.din("rec_w_in", [4096, 11296]); io["dftc"] = P.din("dftc", [256, 512], BF16)
    io["gla_wg"] = P.din("gla_wg", [2, 16, 1536]); io["gla_bgT"] = P.din("gla_bgT", [128, 2, 12])
    io["msel"] = P.din("msel", [4, 2, 5]); io["valid"] = P.din("valid", [128, 2, 5])
    io["on_g"] = P.din("on_g", [1, 512]); io["tabP"] = P.din("tabP", [2, 8192, 2048], BF16)
    io["rec_w_out"] = P.din("rec_w_out", [4096, 4096]); io["final_g"] = P.din("final_g", [1, 4096])
    io["out"] = P.dout("out", [2048, 4096])
    scr("qT", [12, 128, NTOK], BF16); scr("kT", [12, 128, NTOK], BF16); scr("v", [NTOK, 3072], BF16); scr("gdT", [32, NTOK], F32)
    scr("ucs_send", [8, 2048, 256], BF16); scr("ucs_all", [8, 8192, 256], BF16)
    scr("sgl", [NTOK, 3072], BF16); scr("sgfT", [8, 128, NTOK], BF16)
    scr("of_s", [2048, 3072]); scr("ob_s", [2048, 3072])
    scr("sloc_send", [12, 256, 512]); scr("sloc_all", [12, 1024, 512]); scr("la_send", [24, 128]); scr("la_all", [96, 128])
    scr("sctx", [12, 128, 2, 512]); scr("x2", [2048, 4096])
    emit_a1(P, kb, io, G1, Sh1, identb, identf)
    for j in range(8):
        kb.coll("AllGather", ALU.bypass, G4, io["ucs_send"][0][j], io["ucs_send"][1], io["ucs_all"][0][j], io["ucs_all"][1])
    kb.barrier()
    emit_gla(P, kb, io, identb, identf)
    emit_c1(P, kb, io, identb, identf, lambda gm, tmp: load_gate_bcast(P, kb, mod_all, 1, bsel, gm, tmp, 1))

def l1_maps(core, inp):
    b, t = core // 4, core % 4
    msel, valid = gla_core_tables(t)
    bgT = np.stack([inp["gla_bg_f"][0].reshape(12, 128).T, inp["gla_bg_b"][0].reshape(12, 128).T], axis=1)
    return {"rec_w_in": inp["rec_w_in"][0], "dftc": dft_c_table(),
            "gla_wg": np.ascontiguousarray(np.stack([inp["gla_wg_f"][0], inp["gla_wg_b"][0]])), "gla_bgT": np.ascontiguousarray(bgT.astype(np.float32)),
            "msel": msel, "valid": valid, "on_g": inp["gla_on_g"][0][None], "tabP": dft_p_table(t),
            "rec_w_out": inp["rec_w_out"][0], "final_g": inp["final_g"][None]}
def build_fused(upto="all", debug_out=()):
    P = Prog(); kb = P.kb
    io = {}
    def scr(name, shape, dt=F32):
        if name in debug_out:
            io[name] = P.dout(name, shape, dt)
        else:
            io[name] = P.scr(name, shape, dt)
    cT = P.din("cT", [D, 3]); aw = P.din("aw", [D, 3072]); ab = P.din("ab", [1, 3072])
    ngT_in = P.din("ngT", [128, 2, 32]); bsel_in = P.din("bsel", [128, 2])
    io["xin"] = P.din("xin", [NTOK, 4096])
    io["att_w_in"] = P.din("att_w_in", [4096, 8768]); io["gains"] = P.din("gains", [1, 1792])
    io["w_uq"] = P.din("w_uq", [1024, 3072]); io["w_ukv"] = P.din("w_ukv", [512, 4096])
    io["ropeA"] = P.din("ropeA", [NTOK, 128]); io["ropeB"] = P.din("ropeB", [NTOK, 64])
    io["att_w_out"] = P.din("att_w_out", [4096, 4096])
    scr("mod_all", [24, 3072])
    scr("qaT", [16, 128, NTOK], BF16); scr("qbnT", [16, 128, NTOK], BF16); scr("qbrT", [8, 128, NTOK], BF16); scr("gT", [32, 128, NTOK], BF16)
    scr("ktsend", [21, 128, 2048], BF16); scr("ktctx", [21, 128, 256], BF16); scr("vsend", [10, 2048, 256], BF16); scr("vctx", [10, 256, 256], BF16)
    scr("ktall", [21, 512, 2048], BF16); scr("vall", [10, 8192, 256], BF16)
    scr("yT", [32, 128, NTOK], BF16); scr("x1s", [NTOK, 4096])
    identb, identf = make_ident(P, kb)
    ngT = P.sb("ngT", [128, 2, 32], F32); bsel = P.sb("bsel", [128, 2], F32)
    kb.dma("sp", [(ngT[:], ngT_in)], writes=[ngT]); kb.dma("sp", [(bsel[:], bsel_in)], writes=[bsel])
    G = [P.sb(f"G{l}", [128, 2, 32], F32) for l in range(2)]; Sh = [P.sb(f"Sh{l}", [128, 2, 32], F32) for l in range(2)]
    mod_all, mod_all_d = io["mod_all"]
    emit_mod(P, kb, cT, aw, ab, mod_all, mod_all_d)
    with ScopeStack(kb) as sc:
        pt = P.ps("psmt", [128, 512], F32, sc)
        for l in range(2):
            build_mod_tiles(P, kb, mod_all, l, ngT, bsel, identf, pt, G[l], Sh[l])
    if upto == "mod":
        return P.finish()
    emit_a0(P, kb, io, G[0], Sh[0], identb, identf)
    for j in range(21):
        kb.coll("AllGather", ALU.bypass, G4, io["ktsend"][0][j], io["ktsend"][1], io["ktall"][0][j], io["ktall"][1])
    for j in range(10):
        kb.coll("AllGather", ALU.bypass, G4, io["vsend"][0][j], io["vsend"][1], io["vall"][0][j], io["vall"][1])
    kb.barrier()
    emit_b0(P, kb, io, lambda gm, tmp: load_gate_bcast(P, kb, mod_all, 0, bsel, gm, tmp, 2))
    if upto == "l0":
        return P.finish()
    emit_l1(P, kb, io, G[1], Sh[1], identb, identf, mod_all, bsel)
    return P.finish()

def fused_maps(inp, upto="all"):
    x, ctx = inp["x"], inp["ctx"]
    cT = np.ascontiguousarray(np.concatenate([inp["c"], inp["c_ctx"][None]], 0).T)
    awc = np.concatenate([inp["ada_w"][0], inp["ada_w"][1]], axis=1)
    abc = np.concatenate([inp["ada_b"][0], inp["ada_b"][1]], axis=0)[None]
    ngT = np.ascontiguousarray(np.stack([pc(inp["norm_g"][0]), pc(inp["norm_g"][1])], axis=1))
    gains = np.ascontiguousarray(np.concatenate([inp["att_qn_g"][0], inp["att_kn_g"][0], inp["mla_cq_g"][0], inp["mla_ckv_g"][0]])[None])
    maps = []
    for core in range(8):
        b, t = core // 4, core % 4
        bs = np.zeros((128, 2), np.float32); bs[:, b] = 1.0
        m = {"cT": cT, "aw": np.ascontiguousarray(awc[:, core * 3072:(core + 1) * 3072]), "ab": np.ascontiguousarray(abc[:, core * 3072:(core + 1) * 3072]),
             "ngT": ngT, "bsel": bs,
             "xin": np.ascontiguousarray(np.concatenate([x[b, t * 2048:(t + 1) * 2048], ctx[b]], axis=0)),
             "att_w_in": inp["att_w_in"][0], "gains": gains, "w_uq": inp["mla_w_uq"][0], "w_ukv": inp["mla_w_ukv"][0],
             "ropeA": core_rope(t, 128), "ropeB": core_rope(t, 64), "att_w_out": inp["att_w_out"][0]}
        if upto == "all":
            m.update(l1_maps(core, inp))
        maps.append(m)
    return maps

def kernel(**inp):
    inp = {k: np.asarray(v) for k, v in inp.items()}
    nc = build_fused(upto="all")
    maps = fused_maps(inp, upto="all")
    res = run_bass_kernel_spmd(nc, maps, core_ids=list(range(8)), trace=True)
    out = np.stack([np.concatenate([np.asarray(res.results[b * 4 + t]["out"]) for t in range(4)], 0) for b in range(2)])
    return out.astype(np.float32)
```
